# Optimizing a Trainium2 kernel written in Bass

```python
import math
import jax, jax.numpy as jnp
from jax import lax
import numpy as np

D_MODEL = 1024
BATCH = 4
SEQ = 4096
DEPTH = 1

D_MIX = D_MODEL
D_CONV = D_MIX // 2
CONV_WIDTH = 3
N_ATT_HEADS = 4
ATT_VDIM = (D_MIX - D_CONV) // N_ATT_HEADS
ATT_QKDIM = ATT_VDIM // 2
D_QK = N_ATT_HEADS * 2 * ATT_QKDIM
D_V = N_ATT_HEADS * ATT_VDIM
D_IN_PROJ = 3 * D_CONV + 2 * D_QK + D_V
Q_BLOCK = 128
PEER_HEADS = 8
PEER_NKEYS = 128
PEER_NEXPERTS = PEER_NKEYS * PEER_NKEYS
PEER_TOPK = 16
PEER_DQ = 256
PEER_DHALF = PEER_DQ // 2
PEER_CHUNK = 128
LN_EPS = 1e-5
DEEPNORM_ALPHA = (2 * DEPTH) ** 0.25
DEEPNORM_BETA = (8 * DEPTH) ** -0.25

kernel_name = "hymba_conv_diffattn_peer_deepnorm"


def layer_norm(x, g, b):
    xf = x.astype(jnp.float32)
    mu = jnp.mean(xf, axis=-1, keepdims=True)
    var = jnp.mean(jnp.square(xf - mu), axis=-1, keepdims=True)
    y = (xf - mu) * lax.rsqrt(var + LN_EPS) * g.astype(jnp.float32) + b.astype(jnp.float32)
    return y.astype(x.dtype)


def rms_norm(x, g):
    xf = x.astype(jnp.float32)
    y = xf * lax.rsqrt(jnp.mean(jnp.square(xf), axis=-1, keepdims=True) + LN_EPS) * g.astype(jnp.float32)
    return y.astype(x.dtype)


def short_conv(h, w, b):
    s = h.shape[1]
    hp = jnp.pad(h, ((0, 0), (CONV_WIDTH - 1, 0), (0, 0)))
    y = b + w[CONV_WIDTH - 1] * h
    for j in range(CONV_WIDTH - 1):
        y = y + w[j] * hp[:, j:j + s]
    return y


def diff_attention(q, k, v, lam):
    b, s, h, _, d = q.shape
    nblk = s // Q_BLOCK
    scale = 1.0 / math.sqrt(d)
    qb = q.reshape(b, nblk, Q_BLOCK, h, 2, d).transpose(1, 0, 2, 3, 4, 5)
    kpos = jnp.arange(s)

    def one_block(args):
        i, qi = args
        sc = jnp.einsum('bqhpd,bkhpd->bhpqk', qi, k).astype(jnp.float32) * scale
        qpos = i * Q_BLOCK + jnp.arange(Q_BLOCK)
        mask = kpos[None, :] <= qpos[:, None]
        sc = jnp.where(mask, sc, -jnp.inf)
        p = jax.nn.softmax(sc, axis=-1)
        a = p[:, :, 0] - lam * p[:, :, 1]
        return jnp.einsum('bhqk,bkhd->bqhd', a.astype(v.dtype), v)

    out = lax.map(one_block, (jnp.arange(nblk), qb))
    return out.transpose(1, 0, 2, 3, 4).reshape(b, s, h, v.shape[-1])


def peer(x, wq, keys, u, v):
    b, s, d = x.shape
    t = b * s
    xt = x.reshape(t // PEER_CHUNK, PEER_CHUNK, d)

    def one_chunk(xc):
        q = (xc @ wq).reshape(PEER_CHUNK, PEER_HEADS, 2, PEER_DHALF)
        sc = jnp.einsum('chpd,hpnd->chpn', q, keys).astype(jnp.float32)
        s1, i1 = lax.top_k(sc[:, :, 0], PEER_TOPK)
        s2, i2 = lax.top_k(sc[:, :, 1], PEER_TOPK)
        cand = (s1[..., :, None] + s2[..., None, :]).reshape(PEER_CHUNK, PEER_HEADS, PEER_TOPK * PEER_TOPK)
        cidx = (i1[..., :, None] * PEER_NKEYS + i2[..., None, :]).reshape(PEER_CHUNK, PEER_HEADS, PEER_TOPK * PEER_TOPK)
        top_s, pos = lax.top_k(cand, PEER_TOPK)
        eidx = jnp.take_along_axis(cidx, pos, axis=-1)
        g = jax.nn.softmax(top_s, axis=-1).astype(xc.dtype)
        ue = u[eidx]
        act = jax.nn.gelu(jnp.einsum('chkd,cd->chk', ue, xc), approximate=False)
        ve = v[eidx]
        return jnp.einsum('chk,chkd->cd', g * act, ve)

    return lax.map(one_chunk, xt).reshape(b, s, d)


def setup_inputs(seed: int = 0) -> dict:
    key = jax.random.key(seed)
    ks = jax.random.split(key, 20)
    f32 = jnp.float32
    L, D = DEPTH, D_MODEL
    nrm = lambda k, shp: jax.random.normal(k, shp, f32)
    return {
        "x": nrm(ks[0], (BATCH, SEQ, D)),
        "w_in": nrm(ks[1], (L, D, D_IN_PROJ)) * D ** -0.5,
        "conv_w": nrm(ks[2], (L, CONV_WIDTH, D_CONV)) * CONV_WIDTH ** -0.5,
        "conv_b": nrm(ks[3], (L, D_CONV)) * 0.01,
        "lam_q1": nrm(ks[4], (L, ATT_QKDIM)) * 0.1,
        "lam_k1": nrm(ks[5], (L, ATT_QKDIM)) * 0.1,
        "lam_q2": nrm(ks[6], (L, ATT_QKDIM)) * 0.1,
        "lam_k2": nrm(ks[7], (L, ATT_QKDIM)) * 0.1,
        "subln_g": 1.0 + 0.01 * nrm(ks[8], (L, ATT_VDIM)),
        "w_out": nrm(ks[9], (L, D_MIX, D)) * D_MIX ** -0.5 * DEEPNORM_BETA,
        "ln1_g": 1.0 + 0.01 * nrm(ks[10], (L, D)),
        "ln1_b": 0.01 * nrm(ks[11], (L, D)),
        "peer_wq": nrm(ks[12], (L, D, PEER_HEADS * PEER_DQ)) * D ** -0.5,
        "peer_keys": nrm(ks[13], (L, PEER_HEADS, 2, PEER_NKEYS, PEER_DHALF)) * PEER_DHALF ** -0.5,
        "peer_u": nrm(ks[14], (L, PEER_NEXPERTS, D)) * D ** -0.5,
        "peer_v": nrm(ks[15], (L, PEER_NEXPERTS, D)) * DEEPNORM_BETA * PEER_HEADS ** -0.5,
        "ln2_g": 1.0 + 0.01 * nrm(ks[16], (L, D)),
        "ln2_b": 0.01 * nrm(ks[17], (L, D)),
    }


def reference(x, w_in, conv_w, conv_b, lam_q1, lam_k1, lam_q2, lam_k2, subln_g, w_out,
              ln1_g, ln1_b, peer_wq, peer_keys, peer_u, peer_v, ln2_g, ln2_b):
    b, s, _ = x.shape
    splits = [D_CONV, 2 * D_CONV, 3 * D_CONV, 3 * D_CONV + D_QK, 3 * D_CONV + 2 * D_QK]
    for l in range(DEPTH):
        proj = x @ w_in[l]
        gb, gc, hc, q, k, v = jnp.split(proj, splits, axis=-1)
        y_conv = gb * short_conv(gc * hc, conv_w[l], conv_b[l])
        lam_init = 0.8 - 0.6 * math.exp(-0.3 * l)
        lam = (jnp.exp(jnp.sum(lam_q1[l].astype(jnp.float32) * lam_k1[l].astype(jnp.float32)))
               - jnp.exp(jnp.sum(lam_q2[l].astype(jnp.float32) * lam_k2[l].astype(jnp.float32)))
               + lam_init)
        q = q.reshape(b, s, N_ATT_HEADS, 2, ATT_QKDIM)
        k = k.reshape(b, s, N_ATT_HEADS, 2, ATT_QKDIM)
        v = v.reshape(b, s, N_ATT_HEADS, ATT_VDIM)
        y_att = diff_attention(q, k, v, lam)
        y_att = rms_norm(y_att, subln_g[l]) * (1.0 - lam_init)
        y_mix = jnp.concatenate([y_conv, y_att.reshape(b, s, D_V)], axis=-1) @ w_out[l]
        x = layer_norm(DEEPNORM_ALPHA * x + y_mix, ln1_g[l], ln1_b[l])
        y_ffn = peer(x, peer_wq[l], peer_keys[l], peer_u[l], peer_v[l])
        x = layer_norm(DEEPNORM_ALPHA * x + y_ffn, ln2_g[l], ln2_b[l])
    return x
```

```python
import math
from contextlib import ExitStack

import numpy as np
import concourse.bass as bass
import concourse.mybir as mybir
from concourse.bass_utils import run_bass_kernel_spmd

F32 = mybir.dt.float32
BF16 = mybir.dt.bfloat16
U32 = mybir.dt.uint32
AF = mybir.ActivationFunctionType
ALU = mybir.AluOpType
AX = mybir.AxisListType

D = 1024
SEQ = 4096
NT = 2048
NB = 16
LN_EPS = 1e-5
ALPHA = 2.0 ** 0.25
LAM_INIT = 0.8 - 0.6 * math.exp(0.0)
TG = 256


class Buf:
    def __init__(self, name):
        self.name = name
        self.w = None
        self.r = {}


class Sched:
    ENG = ('pe', 'act', 'dve', 'pool', 'sp')

    def __init__(self, nc, es):
        self.nc = nc
        self.es = es
        self.sems = {}
        self.cnt = {}
        for e in self.ENG:
            self.sems[e] = es.enter_context(nc.semaphore('s_' + e))
            self.cnt[e] = 0
        self.seen = {e: {} for e in self.ENG}
        self.prog = {e: [] for e in self.ENG}

    def dma_sem(self, name):
        key = 'd_' + name
        self.sems[key] = self.es.enter_context(self.nc.semaphore(key))
        self.cnt[key] = 0
        return key

    def _waits(self, e, deps):
        for k, v in deps.items():
            if v <= 0:
                continue
            if k == e and e == 'pe':
                continue
            if self.seen[e].get(k, 0) >= v:
                continue
            self.seen[e][k] = v
            sem = self.sems[k]
            self.prog[e].append(lambda eng, sem=sem, v=v: eng.wait_ge(sem, v))

    @staticmethod
    def _deps(reads, writes):
        deps = {}

        def add(k, v):
            if deps.get(k, 0) < v:
                deps[k] = v
        for b in reads:
            if b.w is not None:
                add(*b.w)
        for b in writes:
            if b.w is not None:
                add(*b.w)
            for k, v in b.r.items():
                add(k, v)
        return deps

    @staticmethod
    def _commit(tok, reads, writes):
        k, v = tok
        for b in reads:
            if b.r.get(k, 0) < v:
                b.r[k] = v
        for b in writes:
            b.w = tok
            b.r = {}

    def op(self, e, fn, reads=(), writes=()):
        self._waits(e, self._deps(reads, writes))
        self.cnt[e] += 1
        sem = self.sems[e]
        self.prog[e].append(lambda eng, fn=fn, sem=sem: fn(eng).then_inc(sem, 1))
        self._commit((e, self.cnt[e]), reads, writes)

    def dma(self, e, dkey, fn, reads=(), writes=()):
        b0 = writes[0]
        if not hasattr(b0, "dkey"):
            b0.dkey = self.dma_sem("%s_%d" % (b0.name, len(self.sems)))
        dkey = b0.dkey
        self._waits(e, self._deps(reads, writes))
        self.cnt[dkey] += 16
        sem = self.sems[dkey]
        self.prog[e].append(lambda eng, fn=fn, sem=sem: fn(eng).then_inc(sem, 16))
        self._commit((dkey, self.cnt[dkey]), reads, writes)

    def barrier(self):
        snap = {k: v for k, v in self.cnt.items() if v > 0}
        for e in self.ENG:
            self._waits(e, {k: v for k, v in snap.items() if k != e})

    def final_wait(self, e, bufs):
        self._waits(e, self._deps(bufs, bufs))

    def run(self):
        nc = self.nc
        with nc.Block() as block:
            @block.tensor
            def _(eng):
                for f in self.prog['pe']:
                    f(eng)

            @block.scalar
            def _(eng):
                for f in self.prog['act']:
                    f(eng)

            @block.vector
            def _(eng):
                for f in self.prog['dve']:
                    f(eng)

            @block.gpsimd
            def _(eng):
                for f in self.prog['pool']:
                    f(eng)

            @block.sync
            def _(eng):
                for f in self.prog['sp']:
                    f(eng)


def build(stage=2):
    nc = bass.Bass("TRN2", target_bir_lowering=False)

    def din(name, shape):
        return nc.dram_tensor(name, shape, F32, kind="ExternalInput").ap()

    xT_full = din("xT_full", [D, SEQ])
    xT_mine = din("xT_mine", [D, NT])
    xT_halo = din("xT_halo", [D, 32])
    x_mine = din("x_mine", [NT, D])
    masks_d = din("masks", [128, 4, 128])
    w_in = din("w_in", [D, 3072])
    w_out = din("w_out", [D, D])
    cw_d = din("cw", [128, 4, 4])
    lamv_d = din("lamv", [128, 4, 64])
    sg_d = din("sg", [128, 1])
    ln_d = din("lnp", [128, 4, D])
    if stage >= 2:
        wq_d = din("wq", [D, 2048])
        keysT_d = din("keysT", [128, 16, 128])
        U_d = din("U", [128, 128, 8, 128])
        V_d = din("V", [128, 128, D])
    out_d = nc.dram_tensor("out", [NT, D], F32, kind="ExternalOutput").ap()
    x1_scr = nc.dram_tensor("x1_scr", [NT, D], F32, kind="Internal").ap()
    if stage >= 2:
        Ub = nc.dram_tensor("Ub", [128, 128, 8, 128], BF16, kind="Internal").ap()
        Vb = nc.dram_tensor("Vb", [128, 128, D], BF16, kind="Internal").ap()

    with ExitStack() as es:
        S = Sched(nc, es)

        def sb(ctx, name, shape, dt):
            return ctx.enter_context(nc.sbuf_tensor(name, shape, dt))

        banks = [es.enter_context(nc.psum_tensor("bank%d" % i, [128, 512], F32)) for i in range(8)]
        bbuf = [Buf("bank%d" % i) for i in range(8)]

        ones_bf = sb(es, "ones_bf", [128, 128], BF16); b_ones_bf = Buf("ones_bf")
        ones_f = sb(es, "ones_f", [128, 128], F32); b_ones_f = Buf("ones_f")
        ident_bf = sb(es, "ident_bf", [128, 128], BF16); b_ident_bf = Buf("ident_bf")
        ident_f = sb(es, "ident_f", [128, 128], F32); b_ident_f = Buf("ident_f")
        eps_t = sb(es, "eps_t", [128, 1], F32); b_eps = Buf("eps")
        b_lnp = Buf("lnp")
        lnp_box = [None]
        x1T = sb(es, "x1T", [128, 8, NT], BF16); b_x1T = Buf("x1T")
        d_const = S.dma_sem("const")
        NCV = 8
        b_Ub = [Buf("Ub%d" % k) for k in range(NCV)]; b_Vb = [Buf("Vb%d" % k) for k in range(NCV)]
        d_out = S.dma_sem("out"); b_out = Buf("out")
        d_scr = S.dma_sem("scr"); b_scr = Buf("scr")

        S.op('pool', lambda e: e.memset(ones_bf[:], 1.0), writes=[b_ones_bf])
        S.op('pool', lambda e: e.memset(ones_f[:], 1.0), writes=[b_ones_f])
        S.op('pool', lambda e: e.memset(eps_t[:], LN_EPS), writes=[b_eps])
        S.op('pool', lambda e: e.memset(ident_f[:], 1.0), writes=[b_ident_f])
        S.op('pool', lambda e: e.affine_select(out=ident_f[:], in_=ident_f[:], pattern=[[-1, 128]],
                                               compare_op=ALU.is_equal, fill=0.0, base=0, channel_multiplier=1),
             reads=[b_ident_f], writes=[b_ident_f])
        S.op('pool', lambda e: e.tensor_copy(out=ident_bf[:], in_=ident_f[:]), reads=[b_ident_f], writes=[b_ident_bf])

        def evac(i, out, in_, reads, writes):
            if i % 2 == 0:
                S.op('act', lambda e: e.copy(out=out, in_=in_), reads, writes)
            else:
                S.op('dve', lambda e: e.tensor_copy(out=out, in_=in_), reads, writes)

        def mm(out, lhsT, rhs, start, stop, reads, writes):
            S.op('pe', lambda e: e.matmul(out, lhsT=lhsT, rhs=rhs, start=start, stop=stop), reads, writes)

        def layer_norm(z, zb, gi, out_ap, out_buf, scratch, tail='dve'):
            stats, mv, rstd, b_st = scratch
            for c in range(2):
                S.op('dve', lambda e, c=c: e.bn_stats(out=stats[:, c, :], in_=z[:, c * 512:(c + 1) * 512]),
                     reads=[zb], writes=[b_st])
            S.op('dve', lambda e: e.bn_aggr(out=mv[:], in_=stats[:].rearrange("p c s -> p (c s)")), reads=[b_st], writes=[b_st])
            S.op('act', lambda e: e.activation(out=rstd[:], in_=mv[:, 1:2], func=AF.Sqrt, bias=eps_t[:], scale=1.0),
                 reads=[b_st, b_eps], writes=[b_st])
            S.op('dve', lambda e: e.reciprocal(out=rstd[:], in_=rstd[:]), reads=[b_st], writes=[b_st])
            S.op('dve', lambda e: e.tensor_scalar(out=z[:], in0=z[:], scalar1=mv[:, 0:1], scalar2=rstd[:],
                                                  op0=ALU.subtract, op1=ALU.mult), reads=[zb, b_st], writes=[zb])
            lnp = lnp_box[0]
            S.op(tail, lambda e: e.tensor_tensor(out=z[:], in0=z[:], in1=lnp[:, 0, :], op=ALU.mult),
                 reads=[zb, b_lnp], writes=[zb])
            S.op('dve', lambda e: e.tensor_tensor(out=out_ap, in0=z[:], in1=lnp[:, 1, :], op=ALU.add),
                 reads=[zb, b_lnp], writes=[out_buf])

        ln_stats = sb(es, "ln_stats", [128, 2, 6], F32)
        ln_mv = sb(es, "ln_mv", [128, 2], F32)
        ln_rstd = sb(es, "ln_rstd", [128, 1], F32)
        ln_scratch = (ln_stats, ln_mv, ln_rstd, Buf("ln_scratch"))

        with ExitStack() as pa:
            kT = sb(pa, "kT", [128, 4, SEQ], BF16); b_kT = [Buf("kT%d" % h) for h in range(4)]
            Vt = sb(pa, "Vt", [128, 32, 512], BF16); b_V = [Buf("V%d" % j) for j in range(32)]
            qT0 = sb(pa, "qT0", [128, 4, NT], BF16); b_qT0 = [Buf("qT0_%d" % h) for h in range(4)]
            qT1 = sb(pa, "qT1", [128, 4, NT], BF16); b_qT1 = [Buf("qT1_%d" % h) for h in range(4)]
            ycT = sb(pa, "ycT", [128, 4, NT], BF16); b_ycT = [Buf("ycT%d" % c) for c in range(4)]
            cw = sb(pa, "cw_sb", [128, 4, 4], F32); b_cw = Buf("cw")
            lamv = sb(pa, "lamv_sb", [128, 4, 64], F32); b_lamv = Buf("lamv")
            lam2 = sb(pa, "lam2", [128, 2], F32); b_lam = Buf("lam")
            neglam = sb(pa, "neglam", [128, 1], F32)
            sgs = sb(pa, "sgs", [128, 1], F32); b_sg = Buf("sg")
            mk_f = sb(pa, "mk_f", [128, 4, 128], F32); b_mkf = Buf("mk_f")
            mk = sb(pa, "mk", [128, 4, 128], BF16); b_mk = Buf("mk")
            d_w = S.dma_sem("w"); d_x = S.dma_sem("x")
            lnp1 = sb(pa, "lnp1", [128, 2, D], F32); lnp_box[0] = lnp1
            S.dma('sp', d_const, lambda e: e.dma_start(out=lnp1[:], in_=ln_d[:, 0:2, :]), writes=[b_lnp])

            S.dma('sp', d_const, lambda e: e.dma_start(out=cw[:], in_=cw_d[:, :, :]), writes=[b_cw])
            S.dma('sp', d_const, lambda e: e.dma_start(out=lamv[:], in_=lamv_d[:, :, :]), writes=[b_lamv])
            S.dma('sp', d_const, lambda e: e.dma_start(out=sgs[:], in_=sg_d[:, :]), writes=[b_sg])
            S.dma('sp', d_const, lambda e: e.dma_start(out=mk_f[:], in_=masks_d[:, :, :]), writes=[b_mkf])
            S.op('dve', lambda e: e.tensor_copy(out=mk[:], in_=mk_f[:]), reads=[b_mkf], writes=[b_mk])
            S.op('dve', lambda e: e.tensor_tensor(out=lamv[:, 0, :], in0=lamv[:, 0, :], in1=lamv[:, 1, :], op=ALU.mult),
                 reads=[b_lamv], writes=[b_lamv])
            S.op('dve', lambda e: e.tensor_tensor(out=lamv[:, 2, :], in0=lamv[:, 2, :], in1=lamv[:, 3, :], op=ALU.mult),
                 reads=[b_lamv], writes=[b_lamv])
            S.op('dve', lambda e: e.tensor_reduce(out=lam2[:, 0:1], in_=lamv[:, 0, :], axis=AX.X, op=ALU.add),
                 reads=[b_lamv], writes=[b_lam])
            S.op('dve', lambda e: e.tensor_reduce(out=lam2[:, 1:2], in_=lamv[:, 2, :], axis=AX.X, op=ALU.add),
                 reads=[b_lamv, b_lam], writes=[b_lam])
            S.op('act', lambda e: e.activation(out=lam2[:], in_=lam2[:], func=AF.Exp), reads=[b_lam], writes=[b_lam])
            S.op('dve', lambda e: e.scalar_tensor_tensor(out=neglam[:], in0=lam2[:, 1:2], scalar=-LAM_INIT, in1=lam2[:, 0:1],
                                                         op0=ALU.add, op1=ALU.subtract), reads=[b_lam], writes=[b_lam])
            S.op('dve', lambda e: e.tensor_scalar(out=sgs[:], in0=sgs[:], scalar1=1.0 - LAM_INIT, scalar2=None, op0=ALU.mult),
                 reads=[b_sg], writes=[b_sg])

            with ExitStack() as p1:
                xTf = sb(p1, "xTf", [128, 8, 2048], BF16); b_xTf = [Buf("xTf%d" % t_) for t_ in range(4)]
                wkv = sb(p1, "wkv", [128, 8, 1024], BF16); b_wk = Buf("wk"); b_wv = Buf("wv")
                S.dma('pool', d_w, lambda e: e.dma_start(out=wkv[:, :, 0:512], in_=w_in[:, 2048:2560].rearrange("(c p) n -> p c n", p=128)),
                      writes=[b_wk])

                def load_tile(tt):
                    tl = tt % 4
                    S.dma('pool', d_x, lambda e: e.dma_start(out=xTf[:, :, tl * 512:(tl + 1) * 512],
                                                             in_=xT_full[:, tt * 512:(tt + 1) * 512].rearrange("(c p) n -> p c n", p=128)),
                          writes=[b_xTf[tl]])
                load_tile(0)
                S.dma('pool', d_w, lambda e: e.dma_start(out=wkv[:, :, 512:1024], in_=w_in[:, 2560:3072].rearrange("(c p) n -> p c n", p=128)),
                      writes=[b_wv])
                for tt in range(1, 4):
                    load_tile(tt)
                ev = 0
                for tt in range(8):
                    tl = tt % 4
                    for h in range(4):
                        bk = ev % 4
                        for c in range(8):
                            mm(banks[bk][:, :], wkv[:, c, h * 128:(h + 1) * 128], xTf[:, c, tl * 512:(tl + 1) * 512],
                               c == 0, c == 7, [b_wk, b_xTf[tl]], [bbuf[bk]])
                        evac(ev, kT[:, h, tt * 512:(tt + 1) * 512], banks[bk][:, :], [bbuf[bk]], [b_kT[h]])
                        ev += 1
                    for jl in range(4):
                        j = tt * 4 + jl
                        bk = ev % 4
                        for c in range(8):
                            mm(banks[bk][:, :], xTf[:, c, tl * 512 + jl * 128:tl * 512 + (jl + 1) * 128], wkv[:, c, 512:1024],
                               c == 0, c == 7, [b_wv, b_xTf[tl]], [bbuf[bk]])
                        evac(ev, Vt[:, j, :], banks[bk][:, :], [bbuf[bk]], [b_V[j]])
                        ev += 1
                    if tt + 4 < 8:
                        load_tile(tt + 4)
                S.barrier()

            with ExitStack() as p2:
                HT = NT // 2
                xTm = sb(p2, "xTm", [128, 8, HT], BF16); b_xTm = [Buf("xTm%d" % c) for c in range(8)]
                xTh = sb(p2, "xTh", [128, 8, 32], BF16); b_xTh = Buf("xTh")
                wcq = [sb(p2, "wcq%d" % i, [128, 8, 512], BF16) for i in range(2)]; b_wcq = [Buf("wcq%d" % i) for i in range(2)]
                gcs = sb(p2, "gcs", [128, 512], F32); b_gcs = Buf("gcs")
                gch = sb(p2, "gch", [128, 32], F32); b_gch = Buf("gch")
                hbuf = sb(p2, "hbuf", [128, 4, 130], F32); b_hbuf = Buf("hbuf")
                t1 = sb(p2, "cv_t1", [128, 4, 128], F32); b_t1 = Buf("cv_t1")
                S.op('pool', lambda e: e.memset(qT0[64:128, :, :], 0.0), writes=b_qT0)
                S.op('pool', lambda e: e.memset(qT1[0:64, :, :], 0.0), writes=b_qT1)
                S.dma('pool', d_x, lambda e: e.dma_start(out=xTh[:], in_=xT_halo.rearrange("(c p) n -> p c n", p=128)),
                      writes=[b_xTh])
                ev = 0
                wsel = 0
                for th in range(2):
                    for c in range(8):
                        S.dma('pool', d_x, lambda e, c=c, th=th: e.dma_start(out=xTm[:, c, :], in_=xT_mine[c * 128:(c + 1) * 128, th * HT:(th + 1) * HT]),
                              writes=[b_xTm[c]])
                    w_ = wcq[wsel]; bw_ = b_wcq[wsel]; wsel ^= 1
                    S.dma('pool', d_w, lambda e, w_=w_: e.dma_start(out=w_[:], in_=w_in[:, 1536:2048].rearrange("(c p) n -> p c n", p=128)),
                          writes=[bw_])
                    for h in range(4):
                        for tl in range(2):
                            tt = th * 2 + tl
                            bk = ev % 4
                            for c in range(8):
                                mm(banks[bk][:, :], w_[:, c, h * 128:(h + 1) * 128], xTm[:, c, tl * 512:(tl + 1) * 512],
                                   c == 0, c == 7, [bw_, b_xTm[c]], [bbuf[bk]])
                            S.op('act', lambda e, h=h, tt=tt, bk=bk: e.copy(out=qT0[0:64, h, tt * 512:(tt + 1) * 512], in_=banks[bk][0:64, :]),
                                 [bbuf[bk]], [b_qT0[h]])
                            S.op('dve', lambda e, h=h, tt=tt, bk=bk: e.tensor_copy(out=qT1[64:128, h, tt * 512:(tt + 1) * 512], in_=banks[bk][64:128, :]),
                                 [bbuf[bk]], [b_qT1[h]])
                            ev += 1
                    for fc in range(4):
                        w_ = wcq[wsel]; bw_ = b_wcq[wsel]; wsel ^= 1
                        for g3 in range(3):
                            col = g3 * 512 + fc * 128
                            S.dma('pool', d_w, lambda e, w_=w_, g3=g3, col=col: e.dma_start(
                                out=w_[:, :, g3 * 128:(g3 + 1) * 128], in_=w_in[:, col:col + 128].rearrange("(c p) n -> p c n", p=128)),
                                writes=[bw_])
                        for gi in range(2):
                            for c in range(8):
                                mm(banks[4 + gi][:, 0:32], w_[:, c, (gi + 1) * 128:(gi + 2) * 128], xTh[:, c, :],
                                   c == 0, c == 7, [bw_, b_xTh], [bbuf[4 + gi]])
                        S.op('act', lambda e: e.copy(out=gch[:], in_=banks[4][:, 0:32]), reads=[bbuf[4]], writes=[b_gch])
                        S.op('dve', lambda e: e.tensor_tensor(out=gch[:], in0=gch[:], in1=banks[5][:, 0:32], op=ALU.mult),
                             reads=[b_gch, bbuf[5]], writes=[b_gch])
                        for tl in range(2):
                            tt = th * 2 + tl
                            bs_ = (0, 1, 2) if (fc * 2 + tl) % 2 == 0 else (3, 6, 7)
                            for gi in range(3):
                                for c in range(8):
                                    mm(banks[bs_[gi]][:, :], w_[:, c, gi * 128:(gi + 1) * 128], xTm[:, c, tl * 512:(tl + 1) * 512],
                                       c == 0, c == 7, [bw_, b_xTm[c]], [bbuf[bs_[gi]]])
                            S.op('act', lambda e, bs_=bs_: e.copy(out=gcs[:], in_=banks[bs_[1]][:, :]), reads=[bbuf[bs_[1]]], writes=[b_gcs])
                            S.op('dve', lambda e, bs_=bs_: e.tensor_tensor(out=hbuf[:, :, 2:130], in0=gcs[:].rearrange("p (b t) -> p b t", b=4),
                                                                  in1=banks[bs_[2]][:, :].rearrange("p (b t) -> p b t", b=4), op=ALU.mult),
                                 reads=[b_gcs, bbuf[bs_[2]]], writes=[b_hbuf])
                            S.op('dve', lambda e, tt=tt: e.tensor_copy(out=hbuf[:, :, 0:2],
                                                                       in_=gch[:, tt * 8:(tt + 1) * 8].rearrange("p (b t) -> p b t", b=4)),
                                 reads=[b_gch, b_hbuf], writes=[b_hbuf])
                            S.op('dve', lambda e, fc=fc: e.tensor_scalar(out=t1[:], in0=hbuf[:, :, 2:130], scalar1=cw[:, fc, 2:3], scalar2=cw[:, fc, 3:4],
                                                                         op0=ALU.mult, op1=ALU.add), reads=[b_hbuf, b_cw], writes=[b_t1])
                            S.op('dve', lambda e, fc=fc: e.scalar_tensor_tensor(out=t1[:], in0=hbuf[:, :, 1:129], scalar=cw[:, fc, 1:2], in1=t1[:],
                                                                                op0=ALU.mult, op1=ALU.add), reads=[b_hbuf, b_cw, b_t1], writes=[b_t1])
                            S.op('dve', lambda e, fc=fc: e.scalar_tensor_tensor(out=t1[:], in0=hbuf[:, :, 0:128], scalar=cw[:, fc, 0:1], in1=t1[:],
                                                                                op0=ALU.mult, op1=ALU.add), reads=[b_hbuf, b_cw, b_t1], writes=[b_t1])
                            S.op('dve', lambda e, fc=fc, tt=tt, bs_=bs_: e.tensor_tensor(out=ycT[:, fc, tt * 512:(tt + 1) * 512],
                                                                                in0=t1[:].rearrange("p b t -> p (b t)"), in1=banks[bs_[0]][:, :], op=ALU.mult),
                                 reads=[b_t1, bbuf[bs_[0]]], writes=[b_ycT[fc]])
                S.barrier()

            yaT = sb(pa, "yaT", [128, 4, NT], BF16); b_yaT = [Buf("yaT%d" % c) for c in range(4)]
            with ExitStack() as p3:
                if stage >= 2:
                    for k in range(NCV):
                        i0_, i1_ = k * (128 // NCV), (k + 1) * (128 // NCV)
                        S.dma('pool', None, lambda e, i0_=i0_, i1_=i1_: e.dma_start(
                            out=Ub[i0_:i1_].rearrange("i p c j -> (i p) (c j)"), in_=U_d[i0_:i1_].rearrange("i p c j -> (i p) (c j)")),
                            writes=[b_Ub[k]])
                        S.dma('pool', None, lambda e, i0_=i0_, i1_=i1_: e.dma_start(
                            out=Vb[i0_:i1_].rearrange("i p d -> (i p) d"), in_=V_d[i0_:i1_].rearrange("i p d -> (i p) d")),
                            writes=[b_Vb[k]])
                NPT = 4
                PT = [sb(p3, "PT%d" % i, [128, 512], BF16) for i in range(NPT)]
                b_PT = [Buf("PT%d" % i) for i in range(NPT)]
                Ra = sb(p3, "Ra", [128, 512], F32); b_Ra = Buf("Ra")
                Y1 = sb(p3, "Y1", [128, 512], F32); b_Y1 = Buf("Y1")
                Y2 = sb(p3, "Y2", [128, 512], F32); b_Y2 = Buf("Y2")
                sq = sb(p3, "sq", [128, 512], F32); b_sq = Buf("sq")
                LA = 3
                kk = 0
                pending = []
                allitems = []
                for G in range(4):
                    nj = 8 * G + 8
                    for h in range(4):
                        for j in range(nj):
                            for p in range(2):
                                allitems.append({"G": G, "h": h, "j": j, "p": p, "nj": nj, "idx": j * 2 + p,
                                                 "last": (j == nj - 1 and p == 1)})

                def geom(it):
                    G = it["G"]; j = it["j"]
                    jmaxs = [8 * G + 1, 8 * G + 3, 8 * G + 5, 8 * G + 7]
                    i0 = min(i for i in range(4) if jmaxs[i] >= j)
                    return jmaxs, i0, i0 * 128

                def qk_exp(it):
                    nonlocal kk
                    G, h, j, p = it["G"], it["h"], it["j"], it["p"]
                    jmaxs, i0, c0 = geom(it)
                    sl = kk % NPT
                    kk += 1
                    it["sl"] = sl
                    qc0 = G * 512 + c0
                    ncol = 512 - c0
                    pt = PT[sl]; bpt = b_PT[sl]
                    qTp = qT0 if p == 0 else qT1
                    bqTp = b_qT0[h] if p == 0 else b_qT1[h]
                    mm(banks[sl][:, c0:512], kT[:, h, j * 128:(j + 1) * 128],
                       qTp[:, h, qc0:qc0 + ncol], True, True, [b_kT[h], bqTp], [bbuf[sl]])
                    S.op('act', lambda e: e.activation(out=pt[:, c0:512], in_=banks[sl][:, c0:512], func=AF.Exp, scale=0.125),
                         reads=[bbuf[sl]], writes=[bpt])
                    for i in range(i0, 4):
                        if j in (jmaxs[i] - 1, jmaxs[i]):
                            mi = (i % 2) * 2 + (j - (jmaxs[i] - 1))
                            S.op('dve', lambda e, i=i, mi=mi: e.tensor_tensor(
                                out=pt[:, i * 128:(i + 1) * 128], in0=pt[:, i * 128:(i + 1) * 128], in1=mk[:, mi, :], op=ALU.mult),
                                reads=[bpt, b_mk], writes=[bpt])

                def av_den(it):
                    h, j, p, nj, sl = it["h"], it["j"], it["p"], it["nj"], it["sl"]
                    jmaxs, i0, c0 = geom(it)
                    pt = PT[sl]; bpt = b_PT[sl]
                    mm(banks[4 + p][:, c0:512], Vt[:, j, h * 128:(h + 1) * 128], pt[:, c0:512],
                       j == 0, j == nj - 1, [b_V[j], bpt], [bbuf[4 + p]])
                    mm(banks[6 + p][:, c0:512], ones_bf[:, :], pt[:, c0:512],
                       j == 0, j == nj - 1, [b_ones_bf, bpt], [bbuf[6 + p]])

                def fin1(G, h):
                    S.op('dve', lambda e: e.reciprocal(out=Ra[:], in_=banks[6][:, :]), reads=[bbuf[6]], writes=[b_Ra])
                    S.op('dve', lambda e: e.tensor_tensor(out=Y1[:], in0=Ra[:], in1=banks[4][:, :], op=ALU.mult),
                         reads=[b_Ra, bbuf[4]], writes=[b_Y1])
                    S.op('dve', lambda e: e.reciprocal(out=Ra[:], in_=banks[7][:, :]), reads=[bbuf[7]], writes=[b_Ra])
                    S.op('dve', lambda e: e.tensor_tensor(out=Y2[:], in0=Ra[:], in1=banks[5][:, :], op=ALU.mult),
                         reads=[b_Ra, bbuf[5]], writes=[b_Y2])
                    S.op('dve', lambda e: e.scalar_tensor_tensor(out=Y1[:], in0=Y2[:], scalar=neglam[:], in1=Y1[:],
                                                                 op0=ALU.mult, op1=ALU.add), reads=[b_Y1, b_Y2, b_lam], writes=[b_Y1])
                    S.op('act', lambda e: e.activation(out=sq[:], in_=Y1[:], func=AF.Square), reads=[b_Y1], writes=[b_sq])

                    def fin2(sl):
                        mm(banks[sl][:, :], ones_f[:, :], sq[:, :], True, True, [b_ones_f, b_sq], [bbuf[sl]])
                        S.op('act', lambda e: e.activation(out=sq[:], in_=banks[sl][:, :], func=AF.Sqrt, bias=eps_t[:], scale=1.0 / 128.0),
                             reads=[bbuf[sl], b_eps], writes=[b_sq])
                        S.op('dve', lambda e: e.reciprocal(out=sq[:], in_=sq[:]), reads=[b_sq], writes=[b_sq])
                        S.op('dve', lambda e: e.scalar_tensor_tensor(out=yaT[:, h, G * 512:(G + 1) * 512], in0=Y1[:], scalar=sgs[:], in1=sq[:],
                                                                     op0=ALU.mult, op1=ALU.mult), reads=[b_Y1, b_sg, b_sq], writes=[b_yaT[h]])
                    pending.append(fin2)

                NI = len(allitems)
                for n_ in range(NI + LA):
                    if n_ < NI:
                        qk_exp(allitems[n_])
                    if n_ >= LA:
                        it = allitems[n_ - LA]
                        av_den(it)
                        if pending and it["idx"] == 10:
                            pending.pop()(it["sl"])
                        if it["last"]:
                            fin1(it["G"], it["h"])
                while pending:
                    pending.pop()(0)
                S.barrier()

            with ExitStack() as p4:
                wo = sb(p4, "wo", [128, 8, D], BF16); b_wo = Buf("wo")
                NP4 = 4
                xm = [sb(p4, "xm%d" % i, [128, D], F32) for i in range(NP4)]; b_xm = [Buf("xm%d" % i) for i in range(NP4)]
                kTf = [kT[:, h, :].bitcast(F32) for h in range(4)]
                z = [kTf[i][:, 0:1024] for i in range(NP4)]; b_z = [Buf("z%d" % i) for i in range(NP4)]
                x1o = [kTf[i][:, 1024:2048] for i in range(NP4)]; b_x1o = [Buf("x1o%d" % i) for i in range(NP4)]
                x1b = [Vt[:, 2 * i:2 * i + 2, :].rearrange("p a b -> p (a b)") for i in range(NP4)]; b_x1b = [Buf("x1b%d" % i) for i in range(NP4)]
                lnsc = [(sb(p4, "lns%d" % i, [128, 2, 6], F32), sb(p4, "lnm%d" % i, [128, 2], F32), sb(p4, "lnr%d" % i, [128, 1], F32),
                         Buf("lnsc%d" % i)) for i in range(NP4)]
                d_xm = S.dma_sem("xm")
                S.dma('pool', d_w, lambda e: e.dma_start(out=wo[:], in_=w_out.rearrange("(c p) n -> p c n", p=128)), writes=[b_wo])

                def a4_s0(m):
                    s = m % NP4
                    S.dma('sp', d_xm, lambda e: e.dma_start(out=xm[s][:], in_=x_mine[m * 128:(m + 1) * 128, :]), writes=[b_xm[s]])

                def a4_s1(m):
                    s = m % NP4
                    for nh in range(2):
                        bk = 2 * (m % 3) + nh
                        for c in range(8):
                            src, bsrc = (ycT, b_ycT[c]) if c < 4 else (yaT, b_yaT[c - 4])
                            mm(banks[bk][:, :], src[:, c % 4, m * 128:(m + 1) * 128], wo[:, c, nh * 512:(nh + 1) * 512],
                               c == 0, c == 7, [bsrc, b_wo], [bbuf[bk]])
                        S.op('dve', lambda e, nh=nh, bk=bk: e.scalar_tensor_tensor(
                            out=z[s][:, nh * 512:(nh + 1) * 512], in0=xm[s][:, nh * 512:(nh + 1) * 512], scalar=ALPHA,
                            in1=banks[bk][:, :], op0=ALU.mult, op1=ALU.add), reads=[b_xm[s], bbuf[bk]], writes=[b_z[s]])

                def a4_s2(m):
                    s = m % NP4
                    layer_norm(z[s], b_z[s], 0, x1o[s], b_x1o[s], lnsc[s], tail='pool')
                    if stage == 1:
                        S.dma('sp', d_out, lambda e: e.dma_start(out=out_d[m * 128:(m + 1) * 128, :], in_=x1o[s]),
                              reads=[b_x1o[s]], writes=[b_out])
                    else:
                        S.dma('sp', d_scr, lambda e: e.dma_start(out=x1_scr[m * 128:(m + 1) * 128, :], in_=x1o[s]),
                              reads=[b_x1o[s]], writes=[b_scr])
                    S.op('act', lambda e: e.copy(out=x1b[s], in_=x1o[s]), reads=[b_x1o[s]], writes=[b_x1b[s]])

                def a4_s3(m):
                    s = m % NP4
                    for half in range(2):
                        pv = banks[6 + half][:, :].bitcast(BF16)
                        for cc in range(4):
                            c = half * 4 + cc
                            S.op('pe', lambda e, c=c, cc=cc, pv=pv: e.transpose(out=pv[:, cc * 128:(cc + 1) * 128],
                                                                            in_=x1b[s][:, c * 128:(c + 1) * 128], identity=ident_bf[:]),
                                 reads=[b_x1b[s], b_ident_bf], writes=[bbuf[6 + half]])
                        evac(m + half, x1T[:, half * 4:(half + 1) * 4, m * 128:(m + 1) * 128],
                             pv[:, 0:512].rearrange("p (c t) -> p c t", c=4), [bbuf[6 + half]], [b_x1T])

                a4_s0(0)
                a4_s0(1)
                for t in range(NB + 4):
                    if t + 2 < NB:
                        a4_s0(t + 2)
                    if t >= 4:
                        a4_s3(t - 4)
                    if 2 <= t < NB + 2:
                        a4_s2(t - 2)
                    if t < NB:
                        a4_s1(t)
                S.barrier()

        if stage >= 2:
          with ExitStack() as pb:
            NG = NT // TG
            NCK = TG // 128
            wq = sb(pb, "wq_sb", [128, 8, 2048], BF16); b_wq = Buf("wq")
            keysT = sb(pb, "keysT_sb", [128, 16, 128], BF16); b_keysT = Buf("keysT")
            Gt = sb(pb, "Gt", [128, TG, 128], BF16); b_G = Buf("G")
            NSB = 4
            uT = [sb(pb, "uT%d" % i, [128, 8, 128], BF16) for i in range(NSB)]; b_uT = [Buf("uT%d" % i) for i in range(NSB)]
            vS = [sb(pb, "vS%d" % i, [128, D], BF16) for i in range(NSB)]; b_vS = [Buf("vS%d" % i) for i in range(NSB)]
            qpT = sb(pb, "qpT", [128, 16, TG], BF16); b_qpT = [Buf("qpT%d" % i) for i in range(16)]
            scb = [sb(pb, "scb%d" % i, [128, 4, 128], F32) for i in range(2)]; b_scb = [[Buf("scb%d_%d" % (i, a)) for a in range(4)] for i in range(2)]
            t16 = sb(pb, "t16", [128, 16, 16], F32); b_t16 = [Buf("t16_%d" % i) for i in range(16)]
            ix = sb(pb, "ix", [128, 16, 16], U32); b_ix = [Buf("ix_%d" % i) for i in range(16)]
            ixf = sb(pb, "ixf", [128, 16, 16], F32); b_ixf = Buf("ixf")
            cand = sb(pb, "cand", [128, 8, 256], F32); b_cand = [Buf("cand_%d" % i) for i in range(8)]
            v16 = sb(pb, "v16", [128, 8, 16], F32); b_v16 = [Buf("v16_%d" % i) for i in range(8)]
            pos = sb(pb, "pos", [128, 8, 16], U32); b_pos = [Buf("pos_%d" % i) for i in range(8)]
            posf = sb(pb, "posf", [128, 8, 16], F32); b_posf = Buf("posf")
            af = sb(pb, "af", [128, 8, 16], F32); b_af = Buf("af")
            bf = sb(pb, "bf", [128, 8, 16], F32); b_bf = Buf("bf")
            ev16 = sb(pb, "ev16", [128, 8, 16], F32); b_ev16 = Buf("ev16")
            zs = sb(pb, "zs", [128, 8], F32); b_zs = Buf("zs")
            eq = sb(pb, "eq", [128, 8, 16, 16], BF16); b_eq = Buf("eq")
            tmp = eq; b_tmp = b_eq
            sel = sb(pb, "sel", [128, 3, 128], F32); b_sel = Buf("sel")
            selc = [sb(pb, "selc%d" % i, [128, 3, 128], F32) for i in range(2)]; b_selc = [Buf("selc%d" % i) for i in range(2)]
            selT = [sb(pb, "selT%d" % i, [128, 3, TG], BF16) for i in range(2)]; b_selT = [Buf("selT%d" % i) for i in range(2)]
            iota_i = sb(pb, "iota_i", [128, 128], mybir.dt.int32); b_iota = Buf("iota")
            iota_bf = sb(pb, "iota_bf", [128, 128], BF16)
            iota16 = sb(pb, "iota16", [128, 16], F32)
            thr16 = sb(pb, "thr16", [128, 16], F32)
            NOH = 3
            oh1 = [sb(pb, "oh1_%d" % i, [128, 128], BF16) for i in range(NOH)]; b_oh1 = [Buf("oh1_%d" % i) for i in range(NOH)]
            oh2 = [sb(pb, "oh2_%d" % i, [128, 128], BF16) for i in range(NOH)]; b_oh2 = [Buf("oh2_%d" % i) for i in range(NOH)]
            ga = [sb(pb, "ga%d" % i, [128, TG], BF16) for i in range(2)]; b_ga = [Buf("ga%d" % i) for i in range(2)]
            WT = [sb(pb, "WT%d" % i, [128, TG], BF16) for i in range(2)]; b_WT = [Buf("WT%d" % i) for i in range(2)]
            x1t = sb(pb, "x1t", [128, D], F32); b_x1t = Buf("x1t")
            zt = x1t; b_zt = b_x1t
            ot = x1t; b_ot = b_x1t
            x1t2 = sb(pb, "x1t2", [128, D], F32); b_x1t2 = Buf("x1t2")
            ln_sc2 = (sb(pb, "ln_stats2", [128, 2, 6], F32), sb(pb, "ln_mv2", [128, 2], F32), sb(pb, "ln_rstd2", [128, 1], F32), Buf("ln_scratch2"))
            lnp2 = sb(pb, "lnp2", [128, 2, D], F32); lnp_box[0] = lnp2
            S.dma('sp', None, lambda e: e.dma_start(out=lnp2[:], in_=ln_d[:, 2:4, :]), writes=[b_lnp])

            b_wqc = [Buf("wq%d" % i) for i in range(4)]
            for i in range(4):
                S.dma('pool', None, lambda e, i=i: e.dma_start(out=wq[:, :, i * 512:(i + 1) * 512],
                                                               in_=wq_d[:, i * 512:(i + 1) * 512].rearrange("(c p) n -> p c n", p=128)),
                      writes=[b_wqc[i]])
            S.dma('pool', None, lambda e: e.dma_start(out=keysT[:], in_=keysT_d[:, :, :]), writes=[b_keysT])
            S.op('pool', lambda e: e.iota(iota_i[:], pattern=[[1, 128]], base=0, channel_multiplier=0), writes=[b_iota])
            S.op('dve', lambda e: e.tensor_copy(out=iota_bf[:], in_=iota_i[:]), reads=[b_iota], writes=[b_iota])
            S.op('dve', lambda e: e.tensor_copy(out=iota16[:], in_=iota_i[:, 0:16]), reads=[b_iota], writes=[b_iota])
            S.op('dve', lambda e: e.tensor_scalar(out=thr16[:], in0=iota16[:], scalar1=16.0, scalar2=16.0, op0=ALU.mult, op1=ALU.add),
                 reads=[b_iota], writes=[b_iota])

            def bc4(ap3, axis):
                return ap3.unsqueeze(axis).to_broadcast([128, 8, 16, 16])

            def const4(t):
                a = t[:, :]
                return bass.AP(a.tensor, a.offset, [[a.ap[0][0], 128], [0, 8], [0, 16], [1, 16]])

            t16v = t16[:].rearrange("p (h q) a -> p h q a", q=2)
            ixfv = ixf[:].rearrange("p (h q) a -> p h q a", q=2)
            cand4 = cand[:].rearrange("p h (a b) -> p h a b", a=16)

            def routing_units(tg):
                tk0 = tg * TG
                sT = selT[tg % 2]; b_sT = b_selT[tg % 2]
                units = []

                def u_b1(hp):
                    bk = 6
                    for c in range(8):
                        mm(banks[bk][:, 0:TG], wq[:, c, hp * 128:(hp + 1) * 128], x1T[:, c, tk0:tk0 + TG], c == 0, c == 7,
                           [b_wqc[hp // 4], b_x1T], [bbuf[bk]])
                    S.op('act', lambda e: e.copy(out=qpT[:, hp, :], in_=banks[bk][:, 0:TG]), [bbuf[bk]], [b_qpT[hp]])

                def u_b2(ck, q4):
                    bk = 6
                    sc_ = scb[q4 % 2]; bsc_ = b_scb[q4 % 2]
                    for hl in range(4):
                        hp = q4 * 4 + hl
                        mm(banks[bk][:, hl * 128:(hl + 1) * 128], qpT[:, hp, ck * 128:(ck + 1) * 128], keysT[:, hp, :], True, True,
                           [b_qpT[hp], b_keysT], [bbuf[bk]])
                    S.op('act', lambda e: e.copy(out=sc_[:], in_=banks[bk][:, :].rearrange("p (a n) -> p a n", a=4)), [bbuf[bk]], list(bsc_))
                    hps = [(hl, q4 * 4 + hl) for hl in range(4)]
                    for hl, hp in hps:
                        S.op('dve', lambda e, hp=hp, hl=hl: e.max(out=t16[:, hp, 0:8], in_=sc_[:, hl, :]), reads=[bsc_[hl]], writes=[b_t16[hp]])
                    for hl, hp in hps:
                        S.op('dve', lambda e, hp=hp, hl=hl: e.max_index(out=ix[:, hp, 0:8], in_max=t16[:, hp, 0:8], in_values=sc_[:, hl, :]),
                             reads=[bsc_[hl], b_t16[hp]], writes=[b_ix[hp]])
                    for hl, hp in hps:
                        S.op('dve', lambda e, hp=hp, hl=hl: e.match_replace(out=sc_[:, hl, :], in_to_replace=t16[:, hp, 0:8],
                                                                          in_values=sc_[:, hl, :], imm_value=-1e30),
                             reads=[bsc_[hl], b_t16[hp]], writes=[bsc_[hl]])
                    for hl, hp in hps:
                        S.op('dve', lambda e, hp=hp, hl=hl: e.max(out=t16[:, hp, 8:16], in_=sc_[:, hl, :]), reads=[bsc_[hl]], writes=[b_t16[hp]])
                    for hl, hp in hps:
                        S.op('dve', lambda e, hp=hp, hl=hl: e.max_index(out=ix[:, hp, 8:16], in_max=t16[:, hp, 8:16], in_values=sc_[:, hl, :]),
                             reads=[bsc_[hl], b_t16[hp]], writes=[b_ix[hp]])

                def u_b3a(ck):
                    S.op('dve', lambda e: e.tensor_copy(out=ixf[:], in_=ix[:]), reads=list(b_ix), writes=[b_ixf])
                    S.op('dve', lambda e: e.tensor_tensor(out=cand4, in0=bc4(t16v[:, :, 0, :], 3), in1=bc4(t16v[:, :, 1, :], 2), op=ALU.add),
                         reads=list(b_t16), writes=list(b_cand))

                def u_b3h(ck, st):
                    for h in range(8):
                        if st == 0:
                            S.op('dve', lambda e, h=h: e.max(out=v16[:, h, 0:8], in_=cand[:, h, :]), reads=[b_cand[h]], writes=[b_v16[h]])
                        elif st == 1:
                            S.op('dve', lambda e, h=h: e.max_index(out=pos[:, h, 0:8], in_max=v16[:, h, 0:8], in_values=cand[:, h, :]),
                                 reads=[b_cand[h], b_v16[h]], writes=[b_pos[h]])
                        elif st == 2:
                            S.op('dve', lambda e, h=h: e.match_replace(out=cand[:, h, :], in_to_replace=v16[:, h, 0:8], in_values=cand[:, h, :],
                                                                       imm_value=-1e30), reads=[b_cand[h], b_v16[h]], writes=[b_cand[h]])
                        elif st == 3:
                            S.op('dve', lambda e, h=h: e.max(out=v16[:, h, 8:16], in_=cand[:, h, :]), reads=[b_cand[h]], writes=[b_v16[h]])
                        else:
                            S.op('dve', lambda e, h=h: e.max_index(out=pos[:, h, 8:16], in_max=v16[:, h, 8:16], in_values=cand[:, h, :]),
                                 reads=[b_cand[h], b_v16[h]], writes=[b_pos[h]])

                def u_b3b(ck):
                    S.op('dve', lambda e: e.tensor_copy(out=posf[:], in_=pos[:]), reads=list(b_pos), writes=[b_posf])
                    S.op('dve', lambda e: e.tensor_tensor(out=eq[:], in0=bc4(posf[:, :, :], 3), in1=const4(thr16), op=ALU.is_ge),
                         reads=[b_posf, b_iota], writes=[b_eq])
                    S.op('dve', lambda e: e.tensor_reduce(out=af[:], in_=eq[:], axis=AX.X, op=ALU.add), reads=[b_eq], writes=[b_af])
                    S.op('dve', lambda e: e.scalar_tensor_tensor(out=bf[:], in0=af[:], scalar=-16.0, in1=posf[:], op0=ALU.mult, op1=ALU.add),
                         reads=[b_af, b_posf], writes=[b_bf])
                    for si, (abf, b_abf) in enumerate(((af, b_af), (bf, b_bf))):
                        S.op('dve', lambda e, abf=abf: e.tensor_tensor(out=eq[:], in0=bc4(abf[:, :, :], 3), in1=const4(iota16), op=ALU.is_equal),
                             reads=[b_abf, b_iota], writes=[b_eq])
                        S.op('dve', lambda e, si=si: e.tensor_tensor(out=tmp[:], in0=eq[:], in1=bc4(ixfv[:, :, si, :], 2), op=ALU.mult),
                             reads=[b_eq, b_ixf], writes=[b_tmp])
                        S.op('dve', lambda e, si=si: e.tensor_reduce(out=sel[:, si, :].rearrange("p (h k) -> p h k", h=8), in_=tmp[:], axis=AX.X, op=ALU.add),
                             reads=[b_tmp], writes=[b_sel])
                    S.op('dve', lambda e: e.tensor_tensor(out=ev16[:], in0=v16[:], in1=v16[:, :, 0:1].to_broadcast([128, 8, 16]), op=ALU.subtract),
                         reads=list(b_v16), writes=[b_ev16])

                def u_b3b2(ck):
                    S.op('act', lambda e: e.activation(out=ev16[:], in_=ev16[:], func=AF.Exp), reads=[b_ev16], writes=[b_ev16])
                    S.op('dve', lambda e: e.tensor_reduce(out=zs[:], in_=ev16[:], axis=AX.X, op=ALU.add), reads=[b_ev16], writes=[b_zs])
                    S.op('dve', lambda e: e.reciprocal(out=zs[:], in_=zs[:]), reads=[b_zs], writes=[b_zs])
                    S.op('dve', lambda e: e.tensor_tensor(out=sel[:, 2, :].rearrange("p (h k) -> p h k", h=8), in0=ev16[:],
                                                          in1=zs[:].unsqueeze(2).to_broadcast([128, 8, 16]), op=ALU.mult),
                         reads=[b_ev16, b_zs], writes=[b_sel])
                    S.op('dve', lambda e: e.tensor_copy(out=selc[ck][:], in_=sel[:]), reads=[b_sel], writes=[b_selc[ck]])

                def u_b3c(ck):
                    for si in range(3):
                        S.op('pe', lambda e, si=si: e.transpose(out=banks[6][:, si * 128:(si + 1) * 128], in_=selc[ck][:, si, :], identity=ident_f[:]),
                             reads=[b_selc[ck], b_ident_f], writes=[bbuf[6]])
                    S.op('act', lambda e: e.copy(out=sT[:, :, ck * 128:(ck + 1) * 128],
                                                 in_=banks[6][:, 0:384].rearrange("p (s t) -> p s t", s=3)),
                         reads=[bbuf[6]], writes=[b_sT])

                for hp in range(16):
                    units.append((3 + hp, lambda hp=hp: u_b1(hp)))
                for ck in range(NCK):
                    base = 20 + 44 * ck
                    for q4 in range(4):
                        units.append((base + 5 * q4, lambda ck=ck, q4=q4: u_b2(ck, q4)))
                    units.append((base + 16, lambda ck=ck: u_b3a(ck)))
                    for st in range(5):
                        units.append((base + 17 + st, lambda ck=ck, st=st: u_b3h(ck, st)))
                    units.append((base + 25, lambda ck=ck: u_b3b(ck)))
                    units.append((base + 46, lambda ck=ck: u_b3b2(ck)))
                    units.append((base + 56, lambda ck=ck: u_b3c(ck)))
                units.sort(key=lambda t: t[0])
                return units

            def gen_G(tg):
                sT = selT[tg % 2]; b_sT = b_selT[tg % 2]
                for c in range(TG):
                    sl = c % NOH
                    S.op('dve', lambda e, sl=sl, c=c: e.tensor_scalar(out=oh1[sl][:], in0=iota_bf[:], scalar1=sT[:, 0, c:c + 1],
                                                                      scalar2=sT[:, 2, c:c + 1], op0=ALU.is_equal, op1=ALU.mult),
                         reads=[b_sT, b_iota], writes=[b_oh1[sl]])
                    S.op('dve', lambda e, sl=sl, c=c: e.tensor_scalar(out=oh2[sl][:], in0=iota_bf[:], scalar1=sT[:, 1, c:c + 1],
                                                                      scalar2=None, op0=ALU.is_equal),
                         reads=[b_sT, b_iota], writes=[b_oh2[sl]])
                    gb_ = 6 + (c // 4) % 2
                    mm(banks[gb_][:, (c % 4) * 128:(c % 4 + 1) * 128], oh1[sl][:], oh2[sl][:], True, True, [b_oh1[sl], b_oh2[sl]], [bbuf[gb_]])
                    if c % 4 == 3:
                        S.op('act', lambda e, c=c, gb_=gb_: e.copy(out=Gt[:, c - 3:c + 1, :], in_=banks[gb_][:, :].rearrange("p (t i) -> p t i", t=4)),
                             [bbuf[gb_]], [b_G])

            candv = cand[:].rearrange("p h (a b) -> p (h a) b", a=1).rearrange("p (k q) b -> p k (q b)", k=2)

            def evac_acc():
                for blk in range(NCK):
                    for nh in range(2):
                        S.op('act', lambda e, blk=blk, nh=nh: e.copy(out=candv[:, blk, nh * 512:(nh + 1) * 512], in_=banks[blk * 2 + nh][:, :]),
                             reads=[bbuf[blk * 2 + nh]], writes=list(b_cand))

            def epilogue(tg):
                tk0 = tg * TG
                lnp = lnp2
                blks = []
                for blk in range(NCK):
                    xt_, bx_, sc_ = ((x1t, b_x1t, ln_scratch), (x1t2, b_x1t2, ln_sc2))[blk % 2]
                    blks.append((blk, tk0 + blk * 128, xt_, bx_, sc_))

                def st_load():
                    for blk, r0, xt_, bx_, sc_ in blks:
                        S.dma('sp', None, lambda e, r0=r0, xt_=xt_: e.dma_start(out=xt_[:], in_=x1_scr[r0:r0 + 128, :]), reads=[b_scr], writes=[bx_])

                def st_res():
                    for blk, r0, xt_, bx_, sc_ in blks:
                        S.op('dve', lambda e, blk=blk, xt_=xt_: e.scalar_tensor_tensor(
                            out=xt_[:], in0=xt_[:], scalar=ALPHA, in1=candv[:, blk, :], op0=ALU.mult, op1=ALU.add),
                            reads=[bx_] + list(b_cand), writes=[bx_])

                def st_stats(c):
                    for blk, r0, xt_, bx_, sc_ in blks:
                        S.op('dve', lambda e, xt_=xt_, sc_=sc_: e.bn_stats(out=sc_[0][:, c, :], in_=xt_[:, c * 512:(c + 1) * 512]),
                             reads=[bx_], writes=[sc_[3]])

                def st_aggr():
                    for blk, r0, xt_, bx_, sc_ in blks:
                        S.op('dve', lambda e, sc_=sc_: e.bn_aggr(out=sc_[1][:], in_=sc_[0][:].rearrange("p c s -> p (c s)")), reads=[sc_[3]], writes=[sc_[3]])
                    for blk, r0, xt_, bx_, sc_ in blks:
                        S.op('act', lambda e, sc_=sc_: e.activation(out=sc_[2][:], in_=sc_[1][:, 1:2], func=AF.Sqrt, bias=eps_t[:], scale=1.0),
                             reads=[sc_[3], b_eps], writes=[sc_[3]])

                def st_norm():
                    for blk, r0, xt_, bx_, sc_ in blks:
                        S.op('dve', lambda e, sc_=sc_: e.reciprocal(out=sc_[2][:], in_=sc_[2][:]), reads=[sc_[3]], writes=[sc_[3]])
                    for blk, r0, xt_, bx_, sc_ in blks:
                        S.op('dve', lambda e, xt_=xt_, sc_=sc_: e.tensor_scalar(out=xt_[:], in0=xt_[:], scalar1=sc_[1][:, 0:1], scalar2=sc_[2][:],
                                                                              op0=ALU.subtract, op1=ALU.mult), reads=[bx_, sc_[3]], writes=[bx_])

                def st_gamma():
                    for blk, r0, xt_, bx_, sc_ in blks:
                        S.op('dve', lambda e, xt_=xt_: e.tensor_tensor(out=xt_[:], in0=xt_[:], in1=lnp[:, 0, :], op=ALU.mult),
                             reads=[bx_, b_lnp], writes=[bx_])

                def st_beta():
                    for blk, r0, xt_, bx_, sc_ in blks:
                        S.op('dve', lambda e, xt_=xt_: e.tensor_tensor(out=xt_[:], in0=xt_[:], in1=lnp[:, 1, :], op=ALU.add),
                             reads=[bx_, b_lnp], writes=[bx_])

                def stores():
                    for blk, r0, xt_, bx_, sc_ in blks:
                        S.dma('sp', None, lambda e, r0=r0, xt_=xt_: e.dma_start(out=out_d[r0:r0 + 128, :], in_=xt_[:]), reads=[bx_], writes=[b_out])
                stages = [st_load, st_res, lambda: st_stats(0), lambda: st_stats(1), st_aggr, st_norm, st_gamma, st_beta]
                return stages, stores

            def load_u(i2):
                k_ = i2 % NSB
                kcv = i2 // (128 // NCV)
                S.dma('sp', None, lambda e: e.dma_start(out=uT[k_][:], in_=Ub[i2]), reads=[b_Ub[kcv]], writes=[b_uT[k_]])

            def load_v(i2):
                k_ = i2 % NSB
                kcv = i2 // (128 // NCV)
                S.dma('sp', None, lambda e: e.dma_start(out=vS[k_][:], in_=Vb[i2]), reads=[b_Vb[kcv]], writes=[b_vS[k_]])

            for _, u_ in routing_units(0):
                u_()
            for tg in range(NG):
                tk0 = tg * TG
                for q_ in range(NSB):
                    load_u(q_)
                for q_ in range(NSB):
                    load_v(q_)
                if tg > 0:
                    evac_acc()
                gen_G(tg)
                nxt = routing_units(tg + 1) if tg + 1 < NG else []
                ui = 0

                def emit_A(i2, tk0=tk0):
                    k_ = i2 % NSB; a = i2 % 2; ab = (4, 5, 7)[i2 % 3]
                    for c in range(8):
                        mm(banks[ab][:, 0:TG], uT[k_][:, c, :], x1T[:, c, tk0:tk0 + TG], c == 0, c == 7, [b_uT[k_], b_x1T], [bbuf[ab]])
                    S.op('act', lambda e: e.activation(out=ga[a][:], in_=banks[ab][:, 0:TG], func=AF.Gelu),
                         reads=[bbuf[ab]], writes=[b_ga[a]])

                def emit_W(i2):
                    a = i2 % 2
                    S.op('pool', lambda e: e.tensor_tensor(out=WT[a][:], in0=ga[a][:], in1=Gt[:, :, i2], op=ALU.mult),
                         reads=[b_ga[a], b_G], writes=[b_WT[a]])

                def emit_out(i2):
                    k_ = i2 % NSB; a = i2 % 2
                    for blk in range(NCK):
                        for nh in range(2):
                            mm(banks[blk * 2 + nh][:, :], WT[a][:, blk * 128:(blk + 1) * 128], vS[k_][:, nh * 512:(nh + 1) * 512],
                               i2 == 0, i2 == 127, [b_WT[a], b_vS[k_]], [bbuf[blk * 2 + nh]])

                emit_A(0)
                emit_A(1)
                load_u(4)
                load_u(5)
                emit_W(0)
                for i2 in range(128):
                    if i2 + 2 < 128:
                        emit_A(i2 + 2)
                    if i2 + 1 < 128:
                        emit_W(i2 + 1)
                    emit_out(i2)
                    if i2 + 6 < 128:
                        load_u(i2 + 6)
                    if i2 + 4 < 128:
                        load_v(i2 + 4)
                    if i2 == 0 and tg > 0:
                        epi_stages, epi_stores = epilogue(tg - 1)
                    if tg > 0 and i2 >= 1 and epi_stages:
                        epi_stages.pop(0)()
                    if i2 == 22 and tg > 0:
                        epi_stores()
                    while ui < len(nxt) and nxt[ui][0] <= i2:
                        nxt[ui][1](); ui += 1
                while ui < len(nxt):
                    nxt[ui][1](); ui += 1
            evac_acc()
            fin_stages, fin_stores = epilogue(NG - 1)
            for st_ in fin_stages:
                st_()
            fin_stores()

        S.final_wait('sp', [b_out])
        S.final_wait('act', [b_out])
        S.run()
    return nc


def my_blocks(half):
    out = []
    for g in range(8):
        if half == 0:
            out += [4 * g, 4 * g + 3]
        else:
            out += [4 * g + 1, 4 * g + 2]
    return out


def make_masks(half):
    tri = (np.arange(128)[:, None] <= np.arange(128)[None, :]).astype(np.float32)
    ones = np.ones((128, 128), np.float32)
    zeros = np.zeros((128, 128), np.float32)
    if half == 0:
        m = [tri, zeros, ones, tri]
    else:
        m = [ones, tri, tri, zeros]
    return np.ascontiguousarray(np.stack(m, axis=1))


def prep_inputs(inputs):
    f = lambda a: np.ascontiguousarray(np.asarray(a, dtype=np.float32))
    x = f(inputs["x"])
    w_in = f(inputs["w_in"][0]); w_out = f(inputs["w_out"][0]); wq = f(inputs["peer_wq"][0])
    conv_w = f(inputs["conv_w"][0]); conv_b = f(inputs["conv_b"][0])
    cw = np.zeros((128, 4, 4), np.float32)
    for j in range(3):
        cw[:, :, j] = conv_w[j].reshape(4, 128).T
    cw[:, :, 3] = conv_b.reshape(4, 128).T
    lamv = np.stack([np.broadcast_to(f(inputs[k][0]), (128, 64)) for k in ("lam_q1", "lam_k1", "lam_q2", "lam_k2")], axis=1)
    sg = f(inputs["subln_g"][0]).reshape(128, 1)
    lnp = np.stack([np.broadcast_to(f(inputs[k][0]), (128, D)) for k in ("ln1_g", "ln1_b", "ln2_g", "ln2_b")], axis=1)
    keys = f(inputs["peer_keys"][0])
    keysT = np.ascontiguousarray(keys.reshape(16, 128, 128).transpose(2, 0, 1))
    u = f(inputs["peer_u"][0]).reshape(128, 128, 8, 128)
    U = np.ascontiguousarray(u.transpose(1, 3, 2, 0))
    v = f(inputs["peer_v"][0]).reshape(128, 128, D)
    V = np.ascontiguousarray(v.transpose(1, 0, 2))
    shared = {"w_in": w_in, "w_out": w_out, "cw": cw, "lamv": f(lamv), "sg": sg, "lnp": f(lnp), "wq": wq,
              "keysT": keysT, "U": U, "V": V}
    in_maps = []
    for core in range(8):
        b, half = core // 2, core % 2
        blks = my_blocks(half)
        tok = np.concatenate([np.arange(B * 128, (B + 1) * 128) for B in blks])
        xb = x[b]
        halo = np.zeros((32, D), np.float32)
        for m, B in enumerate(blks):
            if B > 0:
                halo[2 * m:2 * m + 2] = xb[B * 128 - 2:B * 128]
        m = dict(shared)
        m["xT_full"] = np.ascontiguousarray(xb.T)
        m["xT_mine"] = np.ascontiguousarray(xb[tok].T)
        m["xT_halo"] = np.ascontiguousarray(halo.T)
        m["x_mine"] = np.ascontiguousarray(xb[tok])
        m["masks"] = make_masks(half)
        in_maps.append(m)
    return in_maps


def assemble(results):
    out = np.zeros((4, SEQ, D), np.float32)
    for core in range(8):
        b, half = core // 2, core % 2
        blks = my_blocks(half)
        r = results[core]["out"]
        for m, B in enumerate(blks):
            out[b, B * 128:(B + 1) * 128] = r[m * 128:(m + 1) * 128]
    return out


_NC_CACHE = {}


def kernel(**inputs):
    in_maps = prep_inputs(inputs)
    if "nc" not in _NC_CACHE:
        _NC_CACHE["nc"] = build(stage=2)
    res = run_bass_kernel_spmd(_NC_CACHE["nc"], in_maps, core_ids=list(range(8)))
    return assemble(res.results)
```

```python
import math
from contextlib import ExitStack

import numpy as np
import concourse.bass as bass
import concourse.mybir as mybir
from concourse.bass_utils import run_bass_kernel_spmd

F32 = mybir.dt.float32
BF16 = mybir.dt.bfloat16
U32 = mybir.dt.uint32
AF = mybir.ActivationFunctionType
ALU = mybir.AluOpType
AX = mybir.AxisListType

D = 1024
SEQ = 4096
NT = 2048
NB = 16
LN_EPS = 1e-5
ALPHA = 2.0 ** 0.25
LAM_INIT = 0.8 - 0.6 * math.exp(0.0)
TG = 256


class Buf:
    def __init__(self, name):
        self.name = name
        self.w = None
        self.r = {}


class Sched:
    ENG = ('pe', 'act', 'dve', 'pool', 'sp')

    def __init__(self, nc, es):
        self.nc = nc
        self.es = es
        self.sems = {}
        self.cnt = {}
        for e in self.ENG:
            self.sems[e] = es.enter_context(nc.semaphore('s_' + e))
            self.cnt[e] = 0
        self.seen = {e: {} for e in self.ENG}
        self.prog = {e: [] for e in self.ENG}

    def dma_sem(self, name):
        key = 'd_' + name
        self.sems[key] = self.es.enter_context(self.nc.semaphore(key))
        self.cnt[key] = 0
        return key

    def _waits(self, e, deps):
        for k, v in deps.items():
            if v <= 0:
                continue
            if k == e and e == 'pe':
                continue
            if self.seen[e].get(k, 0) >= v:
                continue
            self.seen[e][k] = v
            sem = self.sems[k]
            self.prog[e].append(lambda eng, sem=sem, v=v: eng.wait_ge(sem, v))

    @staticmethod
    def _deps(reads, writes):
        deps = {}

        def add(k, v):
            if deps.get(k, 0) < v:
                deps[k] = v
        for b in reads:
            if b.w is not None:
                add(*b.w)
        for b in writes:
            if b.w is not None:
                add(*b.w)
            for k, v in b.r.items():
                add(k, v)
        return deps

    @staticmethod
    def _commit(tok, reads, writes):
        k, v = tok
        for b in reads:
            if b.r.get(k, 0) < v:
                b.r[k] = v
        for b in writes:
            b.w = tok
            b.r = {}

    def op(self, e, fn, reads=(), writes=()):
        self._waits(e, self._deps(reads, writes))
        self.cnt[e] += 1
        sem = self.sems[e]
        self.prog[e].append(lambda eng, fn=fn, sem=sem: fn(eng).then_inc(sem, 1))
        self._commit((e, self.cnt[e]), reads, writes)

    def dma(self, e, dkey, fn, reads=(), writes=()):
        b0 = writes[0]
        if not hasattr(b0, "dkey"):
            b0.dkey = self.dma_sem("%s_%d" % (b0.name, len(self.sems)))
        dkey = b0.dkey
        self._waits(e, self._deps(reads, writes))
        self.cnt[dkey] += 16
        sem = self.sems[dkey]
        self.prog[e].append(lambda eng, fn=fn, sem=sem: fn(eng).then_inc(sem, 16))
        self._commit((dkey, self.cnt[dkey]), reads, writes)

    def barrier(self):
        snap = {k: v for k, v in self.cnt.items() if v > 0}
        for e in self.ENG:
            self._waits(e, {k: v for k, v in snap.items() if k != e})

    def final_wait(self, e, bufs):
        self._waits(e, self._deps(bufs, bufs))

    def run(self):
        nc = self.nc
        with nc.Block() as block:
            @block.tensor
            def _(eng):
                for f in self.prog['pe']:
                    f(eng)

            @block.scalar
            def _(eng):
                for f in self.prog['act']:
                    f(eng)

            @block.vector
            def _(eng):
                for f in self.prog['dve']:
                    f(eng)

            @block.gpsimd
            def _(eng):
                for f in self.prog['pool']:
                    f(eng)

            @block.sync
            def _(eng):
                for f in self.prog['sp']:
                    f(eng)


def build(stage=2):
    nc = bass.Bass("TRN2", target_bir_lowering=False)

    def din(name, shape):
        return nc.dram_tensor(name, shape, F32, kind="ExternalInput").ap()

    xT_full = din("xT_full", [D, SEQ])
    xT_mine = din("xT_mine", [D, NT])
    xT_halo = din("xT_halo", [D, 32])
    x_mine = din("x_mine", [NT, D])
    masks_d = din("masks", [128, 4, 128])
    w_in = din("w_in", [D, 3072])
    w_out = din("w_out", [D, D])
    cw_d = din("cw", [128, 4, 4])
    lamv_d = din("lamv", [128, 4, 64])
    sg_d = din("sg", [128, 1])
    ln_d = din("lnp", [128, 4, D])
    if stage >= 2:
        wq_d = din("wq", [D, 2048])
        keysT_d = din("keysT", [128, 16, 128])
        U_d = din("U", [128, 128, 8, 128])
        V_d = din("V", [128, 128, D])
    out_d = nc.dram_tensor("out", [NT, D], F32, kind="ExternalOutput").ap()
    x1_scr = nc.dram_tensor("x1_scr", [NT, D], F32, kind="Internal").ap()
    if stage >= 2:
        Ub = nc.dram_tensor("Ub", [128, 128, 8, 128], BF16, kind="Internal").ap()
        Vb = nc.dram_tensor("Vb", [128, 128, D], BF16, kind="Internal").ap()

    with ExitStack() as es:
        S = Sched(nc, es)

        def sb(ctx, name, shape, dt):
            return ctx.enter_context(nc.sbuf_tensor(name, shape, dt))

        banks = [es.enter_context(nc.psum_tensor("bank%d" % i, [128, 512], F32)) for i in range(8)]
        bbuf = [Buf("bank%d" % i) for i in range(8)]

        ones_bf = sb(es, "ones_bf", [128, 128], BF16); b_ones_bf = Buf("ones_bf")
        ones_f = sb(es, "ones_f", [128, 128], F32); b_ones_f = Buf("ones_f")
        ident_bf = sb(es, "ident_bf", [128, 128], BF16); b_ident_bf = Buf("ident_bf")
        ident_f = sb(es, "ident_f", [128, 128], F32); b_ident_f = Buf("ident_f")
        eps_t = sb(es, "eps_t", [128, 1], F32); b_eps = Buf("eps")
        b_lnp = Buf("lnp")
        lnp_box = [None]
        x1T = sb(es, "x1T", [128, 8, NT], BF16); b_x1T = Buf("x1T")
        d_const = S.dma_sem("const")
        NCV = 8
        b_Ub = [Buf("Ub%d" % k) for k in range(NCV)]; b_Vb = [Buf("Vb%d" % k) for k in range(NCV)]
        d_out = S.dma_sem("out"); b_out = Buf("out")
        d_scr = S.dma_sem("scr"); b_scr = Buf("scr")

        S.op('pool', lambda e: e.memset(ones_bf[:], 1.0), writes=[b_ones_bf])
        S.op('pool', lambda e: e.memset(ones_f[:], 1.0), writes=[b_ones_f])
        S.op('pool', lambda e: e.memset(eps_t[:], LN_EPS), writes=[b_eps])
        S.op('pool', lambda e: e.memset(ident_f[:], 1.0), writes=[b_ident_f])
        S.op('pool', lambda e: e.affine_select(out=ident_f[:], in_=ident_f[:], pattern=[[-1, 128]],
                                               compare_op=ALU.is_equal, fill=0.0, base=0, channel_multiplier=1),
             reads=[b_ident_f], writes=[b_ident_f])
        S.op('pool', lambda e: e.tensor_copy(out=ident_bf[:], in_=ident_f[:]), reads=[b_ident_f], writes=[b_ident_bf])

        def evac(i, out, in_, reads, writes):
            if i % 2 == 0:
                S.op('act', lambda e: e.copy(out=out, in_=in_), reads, writes)
            else:
                S.op('dve', lambda e: e.tensor_copy(out=out, in_=in_), reads, writes)

        def mm(out, lhsT, rhs, start, stop, reads, writes):
            S.op('pe', lambda e: e.matmul(out, lhsT=lhsT, rhs=rhs, start=start, stop=stop), reads, writes)

        def layer_norm(z, zb, gi, out_ap, out_buf, scratch, tail='dve'):
            stats, mv, rstd, b_st = scratch
            for c in range(2):
                S.op('dve', lambda e, c=c: e.bn_stats(out=stats[:, c, :], in_=z[:, c * 512:(c + 1) * 512]),
                     reads=[zb], writes=[b_st])
            S.op('dve', lambda e: e.bn_aggr(out=mv[:], in_=stats[:].rearrange("p c s -> p (c s)")), reads=[b_st], writes=[b_st])
            S.op('act', lambda e: e.activation(out=rstd[:], in_=mv[:, 1:2], func=AF.Sqrt, bias=eps_t[:], scale=1.0),
                 reads=[b_st, b_eps], writes=[b_st])
            S.op('dve', lambda e: e.reciprocal(out=rstd[:], in_=rstd[:]), reads=[b_st], writes=[b_st])
            S.op('dve', lambda e: e.tensor_scalar(out=z[:], in0=z[:], scalar1=mv[:, 0:1], scalar2=rstd[:],
                                                  op0=ALU.subtract, op1=ALU.mult), reads=[zb, b_st], writes=[zb])
            lnp = lnp_box[0]
            S.op(tail, lambda e: e.tensor_tensor(out=z[:], in0=z[:], in1=lnp[:, 0, :], op=ALU.mult),
                 reads=[zb, b_lnp], writes=[zb])
            S.op('dve', lambda e: e.tensor_tensor(out=out_ap, in0=z[:], in1=lnp[:, 1, :], op=ALU.add),
                 reads=[zb, b_lnp], writes=[out_buf])

        ln_stats = sb(es, "ln_stats", [128, 2, 6], F32)
        ln_mv = sb(es, "ln_mv", [128, 2], F32)
        ln_rstd = sb(es, "ln_rstd", [128, 1], F32)
        ln_scratch = (ln_stats, ln_mv, ln_rstd, Buf("ln_scratch"))

        with ExitStack() as pa:
            kT = sb(pa, "kT", [128, 4, SEQ], BF16); b_kT = [Buf("kT%d" % h) for h in range(4)]
            Vt = sb(pa, "Vt", [128, 32, 512], BF16); b_V = [Buf("V%d" % j) for j in range(32)]
            qT0 = sb(pa, "qT0", [128, 4, NT], BF16); b_qT0 = [Buf("qT0_%d" % h) for h in range(4)]
            qT1 = sb(pa, "qT1", [128, 4, NT], BF16); b_qT1 = [Buf("qT1_%d" % h) for h in range(4)]
            ycT = sb(pa, "ycT", [128, 4, NT], BF16); b_ycT = [Buf("ycT%d" % c) for c in range(4)]
            cw = sb(pa, "cw_sb", [128, 4, 4], F32); b_cw = Buf("cw")
            lamv = sb(pa, "lamv_sb", [128, 4, 64], F32); b_lamv = Buf("lamv")
            lam2 = sb(pa, "lam2", [128, 2], F32); b_lam = Buf("lam")
            neglam = sb(pa, "neglam", [128, 1], F32)
            sgs = sb(pa, "sgs", [128, 1], F32); b_sg = Buf("sg")
            mk_f = sb(pa, "mk_f", [128, 4, 128], F32); b_mkf = Buf("mk_f")
            mk = sb(pa, "mk", [128, 4, 128], BF16); b_mk = Buf("mk")
            d_w = S.dma_sem("w"); d_x = S.dma_sem("x")
            lnp1 = sb(pa, "lnp1", [128, 2, D], F32); lnp_box[0] = lnp1
            S.dma('sp', d_const, lambda e: e.dma_start(out=lnp1[:], in_=ln_d[:, 0:2, :]), writes=[b_lnp])

            S.dma('sp', d_const, lambda e: e.dma_start(out=cw[:], in_=cw_d[:, :, :]), writes=[b_cw])
            S.dma('sp', d_const, lambda e: e.dma_start(out=lamv[:], in_=lamv_d[:, :, :]), writes=[b_lamv])
            S.dma('sp', d_const, lambda e: e.dma_start(out=sgs[:], in_=sg_d[:, :]), writes=[b_sg])
            S.dma('sp', d_const, lambda e: e.dma_start(out=mk_f[:], in_=masks_d[:, :, :]), writes=[b_mkf])
            S.op('dve', lambda e: e.tensor_copy(out=mk[:], in_=mk_f[:]), reads=[b_mkf], writes=[b_mk])
            S.op('dve', lambda e: e.tensor_tensor(out=lamv[:, 0, :], in0=lamv[:, 0, :], in1=lamv[:, 1, :], op=ALU.mult),
                 reads=[b_lamv], writes=[b_lamv])
            S.op('dve', lambda e: e.tensor_tensor(out=lamv[:, 2, :], in0=lamv[:, 2, :], in1=lamv[:, 3, :], op=ALU.mult),
                 reads=[b_lamv], writes=[b_lamv])
            S.op('dve', lambda e: e.tensor_reduce(out=lam2[:, 0:1], in_=lamv[:, 0, :], axis=AX.X, op=ALU.add),
                 reads=[b_lamv], writes=[b_lam])
            S.op('dve', lambda e: e.tensor_reduce(out=lam2[:, 1:2], in_=lamv[:, 2, :], axis=AX.X, op=ALU.add),
                 reads=[b_lamv, b_lam], writes=[b_lam])
            S.op('act', lambda e: e.activation(out=lam2[:], in_=lam2[:], func=AF.Exp), reads=[b_lam], writes=[b_lam])
            S.op('dve', lambda e: e.scalar_tensor_tensor(out=neglam[:], in0=lam2[:, 1:2], scalar=-LAM_INIT, in1=lam2[:, 0:1],
                                                         op0=ALU.add, op1=ALU.subtract), reads=[b_lam], writes=[b_lam])
            S.op('dve', lambda e: e.tensor_scalar(out=sgs[:], in0=sgs[:], scalar1=1.0 - LAM_INIT, scalar2=None, op0=ALU.mult),
                 reads=[b_sg], writes=[b_sg])

            with ExitStack() as p1:
                xTf = sb(p1, "xTf", [128, 8, 2048], BF16); b_xTf = [Buf("xTf%d" % t_) for t_ in range(4)]
                wkv = sb(p1, "wkv", [128, 8, 1024], BF16); b_wk = Buf("wk"); b_wv = Buf("wv")
                S.dma('pool', d_w, lambda e: e.dma_start(out=wkv[:, :, 0:512], in_=w_in[:, 2048:2560].rearrange("(c p) n -> p c n", p=128)),
                      writes=[b_wk])

                def load_tile(tt):
                    tl = tt % 4
                    S.dma('pool', d_x, lambda e: e.dma_start(out=xTf[:, :, tl * 512:(tl + 1) * 512],
                                                             in_=xT_full[:, tt * 512:(tt + 1) * 512].rearrange("(c p) n -> p c n", p=128)),
                          writes=[b_xTf[tl]])
                load_tile(0)
                S.dma('pool', d_w, lambda e: e.dma_start(out=wkv[:, :, 512:1024], in_=w_in[:, 2560:3072].rearrange("(c p) n -> p c n", p=128)),
                      writes=[b_wv])
                for tt in range(1, 4):
                    load_tile(tt)
                ev = 0
                for tt in range(8):
                    tl = tt % 4
                    for h in range(4):
                        bk = ev % 4
                        for c in range(8):
                            mm(banks[bk][:, :], wkv[:, c, h * 128:(h + 1) * 128], xTf[:, c, tl * 512:(tl + 1) * 512],
                               c == 0, c == 7, [b_wk, b_xTf[tl]], [bbuf[bk]])
                        evac(ev, kT[:, h, tt * 512:(tt + 1) * 512], banks[bk][:, :], [bbuf[bk]], [b_kT[h]])
                        ev += 1
                    for jl in range(4):
                        j = tt * 4 + jl
                        bk = ev % 4
                        for c in range(8):
                            mm(banks[bk][:, :], xTf[:, c, tl * 512 + jl * 128:tl * 512 + (jl + 1) * 128], wkv[:, c, 512:1024],
                               c == 0, c == 7, [b_wv, b_xTf[tl]], [bbuf[bk]])
                        evac(ev, Vt[:, j, :], banks[bk][:, :], [bbuf[bk]], [b_V[j]])
                        ev += 1
                    if tt + 4 < 8:
                        load_tile(tt + 4)
                S.barrier()

            with ExitStack() as p2:
                HT = NT // 2
                xTm = sb(p2, "xTm", [128, 8, HT], BF16); b_xTm = [Buf("xTm%d" % c) for c in range(8)]
                xTh = sb(p2, "xTh", [128, 8, 32], BF16); b_xTh = Buf("xTh")
                wcq = [sb(p2, "wcq%d" % i, [128, 8, 512], BF16) for i in range(2)]; b_wcq = [Buf("wcq%d" % i) for i in range(2)]
                gcs = sb(p2, "gcs", [128, 512], F32); b_gcs = Buf("gcs")
                gch = sb(p2, "gch", [128, 32], F32); b_gch = Buf("gch")
                hbuf = sb(p2, "hbuf", [128, 4, 130], F32); b_hbuf = Buf("hbuf")
                t1 = sb(p2, "cv_t1", [128, 4, 128], F32); b_t1 = Buf("cv_t1")
                S.op('pool', lambda e: e.memset(qT0[64:128, :, :], 0.0), writes=b_qT0)
                S.op('pool', lambda e: e.memset(qT1[0:64, :, :], 0.0), writes=b_qT1)
                S.dma('pool', d_x, lambda e: e.dma_start(out=xTh[:], in_=xT_halo.rearrange("(c p) n -> p c n", p=128)),
                      writes=[b_xTh])
                ev = 0
                wsel = 0
                for th in range(2):
                    for c in range(8):
                        S.dma('pool', d_x, lambda e, c=c, th=th: e.dma_start(out=xTm[:, c, :], in_=xT_mine[c * 128:(c + 1) * 128, th * HT:(th + 1) * HT]),
                              writes=[b_xTm[c]])
                    w_ = wcq[wsel]; bw_ = b_wcq[wsel]; wsel ^= 1
                    S.dma('pool', d_w, lambda e, w_=w_: e.dma_start(out=w_[:], in_=w_in[:, 1536:2048].rearrange("(c p) n -> p c n", p=128)),
                          writes=[bw_])
                    for h in range(4):
                        for tl in range(2):
                            tt = th * 2 + tl
                            bk = ev % 4
                            for c in range(8):
                                mm(banks[bk][:, :], w_[:, c, h * 128:(h + 1) * 128], xTm[:, c, tl * 512:(tl + 1) * 512],
                                   c == 0, c == 7, [bw_, b_xTm[c]], [bbuf[bk]])
                            S.op('act', lambda e, h=h, tt=tt, bk=bk: e.copy(out=qT0[0:64, h, tt * 512:(tt + 1) * 512], in_=banks[bk][0:64, :]),
                                 [bbuf[bk]], [b_qT0[h]])
                            S.op('dve', lambda e, h=h, tt=tt, bk=bk: e.tensor_copy(out=qT1[64:128, h, tt * 512:(tt + 1) * 512], in_=banks[bk][64:128, :]),
                                 [bbuf[bk]], [b_qT1[h]])
                            ev += 1
                    for fc in range(4):
                        w_ = wcq[wsel]; bw_ = b_wcq[wsel]; wsel ^= 1
                        for g3 in range(3):
                            col = g3 * 512 + fc * 128
                            S.dma('pool', d_w, lambda e, w_=w_, g3=g3, col=col: e.dma_start(
                                out=w_[:, :, g3 * 128:(g3 + 1) * 128], in_=w_in[:, col:col + 128].rearrange("(c p) n -> p c n", p=128)),
                                writes=[bw_])
                        for gi in range(2):
                            for c in range(8):
                                mm(banks[4 + gi][:, 0:32], w_[:, c, (gi + 1) * 128:(gi + 2) * 128], xTh[:, c, :],
                                   c == 0, c == 7, [bw_, b_xTh], [bbuf[4 + gi]])
                        S.op('act', lambda e: e.copy(out=gch[:], in_=banks[4][:, 0:32]), reads=[bbuf[4]], writes=[b_gch])
                        S.op('dve', lambda e: e.tensor_tensor(out=gch[:], in0=gch[:], in1=banks[5][:, 0:32], op=ALU.mult),
                             reads=[b_gch, bbuf[5]], writes=[b_gch])
                        for tl in range(2):
                            tt = th * 2 + tl
                            bs_ = (0, 1, 2) if (fc * 2 + tl) % 2 == 0 else (3, 6, 7)
                            for gi in range(3):
                                for c in range(8):
                                    mm(banks[bs_[gi]][:, :], w_[:, c, gi * 128:(gi + 1) * 128], xTm[:, c, tl * 512:(tl + 1) * 512],
                                       c == 0, c == 7, [bw_, b_xTm[c]], [bbuf[bs_[gi]]])
                            S.op('act', lambda e, bs_=bs_: e.copy(out=gcs[:], in_=banks[bs_[1]][:, :]), reads=[bbuf[bs_[1]]], writes=[b_gcs])
                            S.op('dve', lambda e, bs_=bs_: e.tensor_tensor(out=hbuf[:, :, 2:130], in0=gcs[:].rearrange("p (b t) -> p b t", b=4),
                                                                  in1=banks[bs_[2]][:, :].rearrange("p (b t) -> p b t", b=4), op=ALU.mult),
                                 reads=[b_gcs, bbuf[bs_[2]]], writes=[b_hbuf])
                            S.op('dve', lambda e, tt=tt: e.tensor_copy(out=hbuf[:, :, 0:2],
                                                                       in_=gch[:, tt * 8:(tt + 1) * 8].rearrange("p (b t) -> p b t", b=4)),
                                 reads=[b_gch, b_hbuf], writes=[b_hbuf])
                            S.op('dve', lambda e, fc=fc: e.tensor_scalar(out=t1[:], in0=hbuf[:, :, 2:130], scalar1=cw[:, fc, 2:3], scalar2=cw[:, fc, 3:4],
                                                                         op0=ALU.mult, op1=ALU.add), reads=[b_hbuf, b_cw], writes=[b_t1])
                            S.op('dve', lambda e, fc=fc: e.scalar_tensor_tensor(out=t1[:], in0=hbuf[:, :, 1:129], scalar=cw[:, fc, 1:2], in1=t1[:],
                                                                                op0=ALU.mult, op1=ALU.add), reads=[b_hbuf, b_cw, b_t1], writes=[b_t1])
                            S.op('dve', lambda e, fc=fc: e.scalar_tensor_tensor(out=t1[:], in0=hbuf[:, :, 0:128], scalar=cw[:, fc, 0:1], in1=t1[:],
                                                                                op0=ALU.mult, op1=ALU.add), reads=[b_hbuf, b_cw, b_t1], writes=[b_t1])
                            S.op('dve', lambda e, fc=fc, tt=tt, bs_=bs_: e.tensor_tensor(out=ycT[:, fc, tt * 512:(tt + 1) * 512],
                                                                                in0=t1[:].rearrange("p b t -> p (b t)"), in1=banks[bs_[0]][:, :], op=ALU.mult),
                                 reads=[b_t1, bbuf[bs_[0]]], writes=[b_ycT[fc]])
                S.barrier()

            yaT = sb(pa, "yaT", [128, 4, NT], BF16); b_yaT = [Buf("yaT%d" % c) for c in range(4)]
            with ExitStack() as p3:
                if stage >= 2:
                    for k in range(NCV):
                        i0_, i1_ = k * (128 // NCV), (k + 1) * (128 // NCV)
                        S.dma('pool', None, lambda e, i0_=i0_, i1_=i1_: e.dma_start(
                            out=Ub[i0_:i1_].rearrange("i p c j -> (i p) (c j)"), in_=U_d[i0_:i1_].rearrange("i p c j -> (i p) (c j)")),
                            writes=[b_Ub[k]])
                        S.dma('pool', None, lambda e, i0_=i0_, i1_=i1_: e.dma_start(
                            out=Vb[i0_:i1_].rearrange("i p d -> (i p) d"), in_=V_d[i0_:i1_].rearrange("i p d -> (i p) d")),
                            writes=[b_Vb[k]])
                NPT = 4
                PT = [sb(p3, "PT%d" % i, [128, 512], BF16) for i in range(NPT)]
                b_PT = [Buf("PT%d" % i) for i in range(NPT)]
                Ra = sb(p3, "Ra", [128, 512], F32); b_Ra = Buf("Ra")
                Y1 = sb(p3, "Y1", [128, 512], F32); b_Y1 = Buf("Y1")
                Y2 = sb(p3, "Y2", [128, 512], F32); b_Y2 = Buf("Y2")
                sq = sb(p3, "sq", [128, 512], F32); b_sq = Buf("sq")
                LA = 3
                kk = 0
                pending = []
                allitems = []
                for G in range(4):
                    nj = 8 * G + 8
                    for h in range(4):
                        for j in range(nj):
                            for p in range(2):
                                allitems.append({"G": G, "h": h, "j": j, "p": p, "nj": nj, "idx": j * 2 + p,
                                                 "last": (j == nj - 1 and p == 1)})

                def geom(it):
                    G = it["G"]; j = it["j"]
                    jmaxs = [8 * G + 1, 8 * G + 3, 8 * G + 5, 8 * G + 7]
                    i0 = min(i for i in range(4) if jmaxs[i] >= j)
                    return jmaxs, i0, i0 * 128

                def qk_exp(it):
                    nonlocal kk
                    G, h, j, p = it["G"], it["h"], it["j"], it["p"]
                    jmaxs, i0, c0 = geom(it)
                    sl = kk % NPT
                    kk += 1
                    it["sl"] = sl
                    qc0 = G * 512 + c0
                    ncol = 512 - c0
                    pt = PT[sl]; bpt = b_PT[sl]
                    qTp = qT0 if p == 0 else qT1
                    bqTp = b_qT0[h] if p == 0 else b_qT1[h]
                    mm(banks[sl][:, c0:512], kT[:, h, j * 128:(j + 1) * 128],
                       qTp[:, h, qc0:qc0 + ncol], True, True, [b_kT[h], bqTp], [bbuf[sl]])
                    S.op('act', lambda e: e.activation(out=pt[:, c0:512], in_=banks[sl][:, c0:512], func=AF.Exp, scale=0.125),
                         reads=[bbuf[sl]], writes=[bpt])
                    for i in range(i0, 4):
                        if j in (jmaxs[i] - 1, jmaxs[i]):
                            mi = (i % 2) * 2 + (j - (jmaxs[i] - 1))
                            S.op('dve', lambda e, i=i, mi=mi: e.tensor_tensor(
                                out=pt[:, i * 128:(i + 1) * 128], in0=pt[:, i * 128:(i + 1) * 128], in1=mk[:, mi, :], op=ALU.mult),
                                reads=[bpt, b_mk], writes=[bpt])

                def av_den(it):
                    h, j, p, nj, sl = it["h"], it["j"], it["p"], it["nj"], it["sl"]
                    jmaxs, i0, c0 = geom(it)
                    pt = PT[sl]; bpt = b_PT[sl]
                    mm(banks[4 + p][:, c0:512], Vt[:, j, h * 128:(h + 1) * 128], pt[:, c0:512],
                       j == 0, j == nj - 1, [b_V[j], bpt], [bbuf[4 + p]])
                    mm(banks[6 + p][:, c0:512], ones_bf[:, :], pt[:, c0:512],
                       j == 0, j == nj - 1, [b_ones_bf, bpt], [bbuf[6 + p]])

                def fin1(G, h):
                    S.op('dve', lambda e: e.reciprocal(out=Ra[:], in_=banks[6][:, :]), reads=[bbuf[6]], writes=[b_Ra])
                    S.op('dve', lambda e: e.tensor_tensor(out=Y1[:], in0=Ra[:], in1=banks[4][:, :], op=ALU.mult),
                         reads=[b_Ra, bbuf[4]], writes=[b_Y1])
                    S.op('dve', lambda e: e.reciprocal(out=Ra[:], in_=banks[7][:, :]), reads=[bbuf[7]], writes=[b_Ra])
                    S.op('dve', lambda e: e.tensor_tensor(out=Y2[:], in0=Ra[:], in1=banks[5][:, :], op=ALU.mult),
                         reads=[b_Ra, bbuf[5]], writes=[b_Y2])
                    S.op('dve', lambda e: e.scalar_tensor_tensor(out=Y1[:], in0=Y2[:], scalar=neglam[:], in1=Y1[:],
                                                                 op0=ALU.mult, op1=ALU.add), reads=[b_Y1, b_Y2, b_lam], writes=[b_Y1])
                    S.op('act', lambda e: e.activation(out=sq[:], in_=Y1[:], func=AF.Square), reads=[b_Y1], writes=[b_sq])

                    def fin2(sl):
                        mm(banks[sl][:, :], ones_f[:, :], sq[:, :], True, True, [b_ones_f, b_sq], [bbuf[sl]])
                        S.op('act', lambda e: e.activation(out=sq[:], in_=banks[sl][:, :], func=AF.Sqrt, bias=eps_t[:], scale=1.0 / 128.0),
                             reads=[bbuf[sl], b_eps], writes=[b_sq])
                        S.op('dve', lambda e: e.reciprocal(out=sq[:], in_=sq[:]), reads=[b_sq], writes=[b_sq])
                        S.op('dve', lambda e: e.scalar_tensor_tensor(out=yaT[:, h, G * 512:(G + 1) * 512], in0=Y1[:], scalar=sgs[:], in1=sq[:],
                                                                     op0=ALU.mult, op1=ALU.mult), reads=[b_Y1, b_sg, b_sq], writes=[b_yaT[h]])
                    pending.append(fin2)

                NI = len(allitems)
                for n_ in range(NI + LA):
                    if n_ < NI:
                        qk_exp(allitems[n_])
                    if n_ >= LA:
                        it = allitems[n_ - LA]
                        av_den(it)
                        if pending and it["idx"] == 10:
                            pending.pop()(it["sl"])
                        if it["last"]:
                            fin1(it["G"], it["h"])
                while pending:
                    pending.pop()(0)
                S.barrier()

            with ExitStack() as p4:
                wo = sb(p4, "wo", [128, 8, D], BF16); b_wo = Buf("wo")
                NP4 = 4
                xm = [sb(p4, "xm%d" % i, [128, D], F32) for i in range(NP4)]; b_xm = [Buf("xm%d" % i) for i in range(NP4)]
                kTf = [kT[:, h, :].bitcast(F32) for h in range(4)]
                z = [kTf[i][:, 0:1024] for i in range(NP4)]; b_z = [Buf("z%d" % i) for i in range(NP4)]
                x1o = [kTf[i][:, 1024:2048] for i in range(NP4)]; b_x1o = [Buf("x1o%d" % i) for i in range(NP4)]
                x1b = [Vt[:, 2 * i:2 * i + 2, :].rearrange("p a b -> p (a b)") for i in range(NP4)]; b_x1b = [Buf("x1b%d" % i) for i in range(NP4)]
                lnsc = [(sb(p4, "lns%d" % i, [128, 2, 6], F32), sb(p4, "lnm%d" % i, [128, 2], F32), sb(p4, "lnr%d" % i, [128, 1], F32),
                         Buf("lnsc%d" % i)) for i in range(NP4)]
                d_xm = S.dma_sem("xm")
                S.dma('pool', d_w, lambda e: e.dma_start(out=wo[:], in_=w_out.rearrange("(c p) n -> p c n", p=128)), writes=[b_wo])

                def a4_s0(m):
                    s = m % NP4
                    S.dma('sp', d_xm, lambda e: e.dma_start(out=xm[s][:], in_=x_mine[m * 128:(m + 1) * 128, :]), writes=[b_xm[s]])

                def a4_s1(m):
                    s = m % NP4
                    for nh in range(2):
                        bk = 2 * (m % 3) + nh
                        for c in range(8):
                            src, bsrc = (ycT, b_ycT[c]) if c < 4 else (yaT, b_yaT[c - 4])
                            mm(banks[bk][:, :], src[:, c % 4, m * 128:(m + 1) * 128], wo[:, c, nh * 512:(nh + 1) * 512],
                               c == 0, c == 7, [bsrc, b_wo], [bbuf[bk]])
                        S.op('dve', lambda e, nh=nh, bk=bk: e.scalar_tensor_tensor(
                            out=z[s][:, nh * 512:(nh + 1) * 512], in0=xm[s][:, nh * 512:(nh + 1) * 512], scalar=ALPHA,
                            in1=banks[bk][:, :], op0=ALU.mult, op1=ALU.add), reads=[b_xm[s], bbuf[bk]], writes=[b_z[s]])

                def a4_s2(m):
                    s = m % NP4
                    layer_norm(z[s], b_z[s], 0, x1o[s], b_x1o[s], lnsc[s], tail='pool')
                    if stage == 1:
                        S.dma('sp', d_out, lambda e: e.dma_start(out=out_d[m * 128:(m + 1) * 128, :], in_=x1o[s]),
                              reads=[b_x1o[s]], writes=[b_out])
                    else:
                        S.dma('sp', d_scr, lambda e: e.dma_start(out=x1_scr[m * 128:(m + 1) * 128, :], in_=x1o[s]),
                              reads=[b_x1o[s]], writes=[b_scr])
                    S.op('act', lambda e: e.copy(out=x1b[s], in_=x1o[s]), reads=[b_x1o[s]], writes=[b_x1b[s]])

                def a4_s3(m):
                    s = m % NP4
                    for half in range(2):
                        pv = banks[6 + half][:, :].bitcast(BF16)
                        for cc in range(4):
                            c = half * 4 + cc
                            S.op('pe', lambda e, c=c, cc=cc, pv=pv: e.transpose(out=pv[:, cc * 128:(cc + 1) * 128],
                                                                            in_=x1b[s][:, c * 128:(c + 1) * 128], identity=ident_bf[:]),
                                 reads=[b_x1b[s], b_ident_bf], writes=[bbuf[6 + half]])
                        evac(m + half, x1T[:, half * 4:(half + 1) * 4, m * 128:(m + 1) * 128],
                             pv[:, 0:512].rearrange("p (c t) -> p c t", c=4), [bbuf[6 + half]], [b_x1T])

                a4_s0(0)
                a4_s0(1)
                for t in range(NB + 4):
                    if t + 2 < NB:
                        a4_s0(t + 2)
                    if t >= 4:
                        a4_s3(t - 4)
                    if 2 <= t < NB + 2:
                        a4_s2(t - 2)
                    if t < NB:
                        a4_s1(t)
                S.barrier()

        if stage >= 2:
          with ExitStack() as pb:
            NG = NT // TG
            NCK = TG // 128
            wq = sb(pb, "wq_sb", [128, 8, 2048], BF16); b_wq = Buf("wq")
            keysT = sb(pb, "keysT_sb", [128, 16, 128], BF16); b_keysT = Buf("keysT")
            Gt = sb(pb, "Gt", [128, TG, 128], BF16); b_G = Buf("G")
            NSB = 4
            uT = [sb(pb, "uT%d" % i, [128, 8, 128], BF16) for i in range(NSB)]; b_uT = [Buf("uT%d" % i) for i in range(NSB)]
            vS = [sb(pb, "vS%d" % i, [128, D], BF16) for i in range(NSB)]; b_vS = [Buf("vS%d" % i) for i in range(NSB)]
            qpT = sb(pb, "qpT", [128, 16, TG], BF16); b_qpT = [Buf("qpT%d" % i) for i in range(16)]
            scb = [sb(pb, "scb%d" % i, [128, 4, 128], F32) for i in range(2)]; b_scb = [[Buf("scb%d_%d" % (i, a)) for a in range(4)] for i in range(2)]
            t16 = sb(pb, "t16", [128, 16, 16], F32); b_t16 = [Buf("t16_%d" % i) for i in range(16)]
            ix = sb(pb, "ix", [128, 16, 16], U32); b_ix = [Buf("ix_%d" % i) for i in range(16)]
            ixf = sb(pb, "ixf", [128, 16, 16], F32); b_ixf = Buf("ixf")
            cand = sb(pb, "cand", [128, 8, 256], F32); b_cand = [Buf("cand_%d" % i) for i in range(8)]
            v16 = sb(pb, "v16", [128, 8, 16], F32); b_v16 = [Buf("v16_%d" % i) for i in range(8)]
            pos = sb(pb, "pos", [128, 8, 16], U32); b_pos = [Buf("pos_%d" % i) for i in range(8)]
            posf = sb(pb, "posf", [128, 8, 16], F32); b_posf = Buf("posf")
            af = sb(pb, "af", [128, 8, 16], F32); b_af = Buf("af")
            bf = sb(pb, "bf", [128, 8, 16], F32); b_bf = Buf("bf")
            ev16 = sb(pb, "ev16", [128, 8, 16], F32); b_ev16 = Buf("ev16")
            zs = sb(pb, "zs", [128, 8], F32); b_zs = Buf("zs")
            eq = sb(pb, "eq", [128, 8, 16, 16], BF16); b_eq = Buf("eq")
            tmp = eq; b_tmp = b_eq
            sel = sb(pb, "sel", [128, 3, 128], F32); b_sel = Buf("sel")
            selc = [sb(pb, "selc%d" % i, [128, 3, 128], F32) for i in range(2)]; b_selc = [Buf("selc%d" % i) for i in range(2)]
            selT = [sb(pb, "selT%d" % i, [128, 3, TG], BF16) for i in range(2)]; b_selT = [Buf("selT%d" % i) for i in range(2)]
            iota_i = sb(pb, "iota_i", [128, 128], mybir.dt.int32); b_iota = Buf("iota")
            iota_bf = sb(pb, "iota_bf", [128, 128], BF16)
            iota16 = sb(pb, "iota16", [128, 16], F32)
            thr16 = sb(pb, "thr16", [128, 16], F32)
            NOH = 3
            oh1 = [sb(pb, "oh1_%d" % i, [128, 128], BF16) for i in range(NOH)]; b_oh1 = [Buf("oh1_%d" % i) for i in range(NOH)]
            oh2 = [sb(pb, "oh2_%d" % i, [128, 128], BF16) for i in range(NOH)]; b_oh2 = [Buf("oh2_%d" % i) for i in range(NOH)]
            ga = [sb(pb, "ga%d" % i, [128, TG], BF16) for i in range(2)]; b_ga = [Buf("ga%d" % i) for i in range(2)]
            WT = [sb(pb, "WT%d" % i, [128, TG], BF16) for i in range(2)]; b_WT = [Buf("WT%d" % i) for i in range(2)]
            x1t = sb(pb, "x1t", [128, D], F32); b_x1t = Buf("x1t")
            zt = x1t; b_zt = b_x1t
            ot = x1t; b_ot = b_x1t
            x1t2 = sb(pb, "x1t2", [128, D], F32); b_x1t2 = Buf("x1t2")
            ln_sc2 = (sb(pb, "ln_stats2", [128, 2, 6], F32), sb(pb, "ln_mv2", [128, 2], F32), sb(pb, "ln_rstd2", [128, 1], F32), Buf("ln_scratch2"))
            lnp2 = sb(pb, "lnp2", [128, 2, D], F32); lnp_box[0] = lnp2
            S.dma('sp', None, lambda e: e.dma_start(out=lnp2[:], in_=ln_d[:, 2:4, :]), writes=[b_lnp])

            b_wqc = [Buf("wq%d" % i) for i in range(4)]
            for i in range(4):
                S.dma('pool', None, lambda e, i=i: e.dma_start(out=wq[:, :, i * 512:(i + 1) * 512],
                                                               in_=wq_d[:, i * 512:(i + 1) * 512].rearrange("(c p) n -> p c n", p=128)),
                      writes=[b_wqc[i]])
            S.dma('pool', None, lambda e: e.dma_start(out=keysT[:], in_=keysT_d[:, :, :]), writes=[b_keysT])
            S.op('pool', lambda e: e.iota(iota_i[:], pattern=[[1, 128]], base=0, channel_multiplier=0), writes=[b_iota])
            S.op('dve', lambda e: e.tensor_copy(out=iota_bf[:], in_=iota_i[:]), reads=[b_iota], writes=[b_iota])
            S.op('dve', lambda e: e.tensor_copy(out=iota16[:], in_=iota_i[:, 0:16]), reads=[b_iota], writes=[b_iota])
            S.op('dve', lambda e: e.tensor_scalar(out=thr16[:], in0=iota16[:], scalar1=16.0, scalar2=16.0, op0=ALU.mult, op1=ALU.add),
                 reads=[b_iota], writes=[b_iota])

            def bc4(ap3, axis):
                return ap3.unsqueeze(axis).to_broadcast([128, 8, 16, 16])

            def const4(t):
                a = t[:, :]
                return bass.AP(a.tensor, a.offset, [[a.ap[0][0], 128], [0, 8], [0, 16], [1, 16]])

            t16v = t16[:].rearrange("p (h q) a -> p h q a", q=2)
            ixfv = ixf[:].rearrange("p (h q) a -> p h q a", q=2)
            cand4 = cand[:].rearrange("p h (a b) -> p h a b", a=16)

            def routing_units(tg):
                tk0 = tg * TG
                sT = selT[tg % 2]; b_sT = b_selT[tg % 2]
                units = []

                def u_b1(hp):
                    bk = 6
                    for c in range(8):
                        mm(banks[bk][:, 0:TG], wq[:, c, hp * 128:(hp + 1) * 128], x1T[:, c, tk0:tk0 + TG], c == 0, c == 7,
                           [b_wqc[hp // 4], b_x1T], [bbuf[bk]])
                    S.op('act', lambda e: e.copy(out=qpT[:, hp, :], in_=banks[bk][:, 0:TG]), [bbuf[bk]], [b_qpT[hp]])

                def u_b2(ck, q4):
                    bk = 6
                    sc_ = scb[q4 % 2]; bsc_ = b_scb[q4 % 2]
                    for hl in range(4):
                        hp = q4 * 4 + hl
                        mm(banks[bk][:, hl * 128:(hl + 1) * 128], qpT[:, hp, ck * 128:(ck + 1) * 128], keysT[:, hp, :], True, True,
                           [b_qpT[hp], b_keysT], [bbuf[bk]])
                    S.op('act', lambda e: e.copy(out=sc_[:], in_=banks[bk][:, :].rearrange("p (a n) -> p a n", a=4)), [bbuf[bk]], list(bsc_))
                    hps = [(hl, q4 * 4 + hl) for hl in range(4)]
                    for hl, hp in hps:
                        S.op('dve', lambda e, hp=hp, hl=hl: e.max(out=t16[:, hp, 0:8], in_=sc_[:, hl, :]), reads=[bsc_[hl]], writes=[b_t16[hp]])
                    for hl, hp in hps:
                        S.op('dve', lambda e, hp=hp, hl=hl: e.max_index(out=ix[:, hp, 0:8], in_max=t16[:, hp, 0:8], in_values=sc_[:, hl, :]),
                             reads=[bsc_[hl], b_t16[hp]], writes=[b_ix[hp]])
                    for hl, hp in hps:
                        S.op('dve', lambda e, hp=hp, hl=hl: e.match_replace(out=sc_[:, hl, :], in_to_replace=t16[:, hp, 0:8],
                                                                          in_values=sc_[:, hl, :], imm_value=-1e30),
                             reads=[bsc_[hl], b_t16[hp]], writes=[bsc_[hl]])
                    for hl, hp in hps:
                        S.op('dve', lambda e, hp=hp, hl=hl: e.max(out=t16[:, hp, 8:16], in_=sc_[:, hl, :]), reads=[bsc_[hl]], writes=[b_t16[hp]])
                    for hl, hp in hps:
                        S.op('dve', lambda e, hp=hp, hl=hl: e.max_index(out=ix[:, hp, 8:16], in_max=t16[:, hp, 8:16], in_values=sc_[:, hl, :]),
                             reads=[bsc_[hl], b_t16[hp]], writes=[b_ix[hp]])

                def u_b3a(ck):
                    S.op('dve', lambda e: e.tensor_copy(out=ixf[:], in_=ix[:]), reads=list(b_ix), writes=[b_ixf])
                    S.op('dve', lambda e: e.tensor_tensor(out=cand4, in0=bc4(t16v[:, :, 0, :], 3), in1=bc4(t16v[:, :, 1, :], 2), op=ALU.add),
                         reads=list(b_t16), writes=list(b_cand))

                def u_b3h(ck, st):
                    for h in range(8):
                        if st == 0:
                            S.op('dve', lambda e, h=h: e.max(out=v16[:, h, 0:8], in_=cand[:, h, :]), reads=[b_cand[h]], writes=[b_v16[h]])
                        elif st == 1:
                            S.op('dve', lambda e, h=h: e.max_index(out=pos[:, h, 0:8], in_max=v16[:, h, 0:8], in_values=cand[:, h, :]),
                                 reads=[b_cand[h], b_v16[h]], writes=[b_pos[h]])
                        elif st == 2:
                            S.op('dve', lambda e, h=h: e.match_replace(out=cand[:, h, :], in_to_replace=v16[:, h, 0:8], in_values=cand[:, h, :],
                                                                       imm_value=-1e30), reads=[b_cand[h], b_v16[h]], writes=[b_cand[h]])
                        elif st == 3:
                            S.op('dve', lambda e, h=h: e.max(out=v16[:, h, 8:16], in_=cand[:, h, :]), reads=[b_cand[h]], writes=[b_v16[h]])
                        else:
                            S.op('dve', lambda e, h=h: e.max_index(out=pos[:, h, 8:16], in_max=v16[:, h, 8:16], in_values=cand[:, h, :]),
                                 reads=[b_cand[h], b_v16[h]], writes=[b_pos[h]])

                def u_b3b(ck, part):
                    if part == 0:
                        S.op('dve', lambda e: e.tensor_copy(out=posf[:], in_=pos[:]), reads=list(b_pos), writes=[b_posf])
                        S.op('dve', lambda e: e.tensor_tensor(out=eq[:], in0=bc4(posf[:, :, :], 3), in1=const4(thr16), op=ALU.is_ge),
                             reads=[b_posf, b_iota], writes=[b_eq])
                    elif part == 1:
                        S.op('dve', lambda e: e.tensor_reduce(out=af[:], in_=eq[:], axis=AX.X, op=ALU.add), reads=[b_eq], writes=[b_af])
                        S.op('dve', lambda e: e.scalar_tensor_tensor(out=bf[:], in0=af[:], scalar=-16.0, in1=posf[:], op0=ALU.mult, op1=ALU.add),
                             reads=[b_af, b_posf], writes=[b_bf])
                    elif part in (2, 5):
                        abf, b_abf = ((af, b_af), (bf, b_bf))[0 if part == 2 else 1]
                        S.op('dve', lambda e: e.tensor_tensor(out=eq[:], in0=bc4(abf[:, :, :], 3), in1=const4(iota16), op=ALU.is_equal),
                             reads=[b_abf, b_iota], writes=[b_eq])
                    elif part in (3, 6):
                        si = 0 if part == 3 else 1
                        S.op('dve', lambda e: e.tensor_tensor(out=tmp[:], in0=eq[:], in1=bc4(ixfv[:, :, si, :], 2), op=ALU.mult),
                             reads=[b_eq, b_ixf], writes=[b_tmp])
                    elif part in (4, 7):
                        si = 0 if part == 4 else 1
                        S.op('dve', lambda e: e.tensor_reduce(out=sel[:, si, :].rearrange("p (h k) -> p h k", h=8), in_=tmp[:], axis=AX.X, op=ALU.add),
                             reads=[b_tmp], writes=[b_sel])
                        if part == 7:
                            S.op('dve', lambda e: e.tensor_tensor(out=ev16[:], in0=v16[:], in1=v16[:, :, 0:1].to_broadcast([128, 8, 16]), op=ALU.subtract),
                                 reads=list(b_v16), writes=[b_ev16])

                def u_b3b2(ck):
                    S.op('act', lambda e: e.activation(out=ev16[:], in_=ev16[:], func=AF.Exp), reads=[b_ev16], writes=[b_ev16])
                    S.op('dve', lambda e: e.tensor_reduce(out=zs[:], in_=ev16[:], axis=AX.X, op=ALU.add), reads=[b_ev16], writes=[b_zs])
                    S.op('dve', lambda e: e.reciprocal(out=zs[:], in_=zs[:]), reads=[b_zs], writes=[b_zs])
                    S.op('dve', lambda e: e.tensor_tensor(out=sel[:, 2, :].rearrange("p (h k) -> p h k", h=8), in0=ev16[:],
                                                          in1=zs[:].unsqueeze(2).to_broadcast([128, 8, 16]), op=ALU.mult),
                         reads=[b_ev16, b_zs], writes=[b_sel])
                    S.op('dve', lambda e: e.tensor_copy(out=selc[ck][:], in_=sel[:]), reads=[b_sel], writes=[b_selc[ck]])

                def u_b3c(ck):
                    for si in range(3):
                        S.op('pe', lambda e, si=si: e.transpose(out=banks[6][:, si * 128:(si + 1) * 128], in_=selc[ck][:, si, :], identity=ident_f[:]),
                             reads=[b_selc[ck], b_ident_f], writes=[bbuf[6]])
                    S.op('act', lambda e: e.copy(out=sT[:, :, ck * 128:(ck + 1) * 128],
                                                 in_=banks[6][:, 0:384].rearrange("p (s t) -> p s t", s=3)),
                         reads=[bbuf[6]], writes=[b_sT])

                for hp in range(16):
                    units.append((3 + hp, lambda hp=hp: u_b1(hp)))
                for ck in range(NCK):
                    base = 20 + 44 * ck
                    for q4 in range(4):
                        units.append((base + 5 * q4, lambda ck=ck, q4=q4: u_b2(ck, q4)))
                    units.append((base + 16, lambda ck=ck: u_b3a(ck)))
                    for st in range(5):
                        units.append((base + 17 + st, lambda ck=ck, st=st: u_b3h(ck, st)))
                    for part in range(8):
                        units.append((base + 25 + 2 * part, lambda ck=ck, part=part: u_b3b(ck, part)))
                    units.append((base + 46, lambda ck=ck: u_b3b2(ck)))
                    units.append((base + 56, lambda ck=ck: u_b3c(ck)))
                units.sort(key=lambda t: t[0])
                return units

            def gen_G(tg):
                sT = selT[tg % 2]; b_sT = b_selT[tg % 2]
                for c in range(TG):
                    sl = c % NOH
                    S.op('dve', lambda e, sl=sl, c=c: e.tensor_scalar(out=oh1[sl][:], in0=iota_bf[:], scalar1=sT[:, 0, c:c + 1],
                                                                      scalar2=sT[:, 2, c:c + 1], op0=ALU.is_equal, op1=ALU.mult),
                         reads=[b_sT, b_iota], writes=[b_oh1[sl]])
                    S.op('dve', lambda e, sl=sl, c=c: e.tensor_scalar(out=oh2[sl][:], in0=iota_bf[:], scalar1=sT[:, 1, c:c + 1],
                                                                      scalar2=None, op0=ALU.is_equal),
                         reads=[b_sT, b_iota], writes=[b_oh2[sl]])
                    gb_ = 6 + (c // 4) % 2
                    mm(banks[gb_][:, (c % 4) * 128:(c % 4 + 1) * 128], oh1[sl][:], oh2[sl][:], True, True, [b_oh1[sl], b_oh2[sl]], [bbuf[gb_]])
                    if c % 4 == 3:
                        S.op('act', lambda e, c=c, gb_=gb_: e.copy(out=Gt[:, c - 3:c + 1, :], in_=banks[gb_][:, :].rearrange("p (t i) -> p t i", t=4)),
                             [bbuf[gb_]], [b_G])

            candv = cand[:].rearrange("p h (a b) -> p (h a) b", a=1).rearrange("p (k q) b -> p k (q b)", k=2)

            def evac_acc():
                for blk in range(NCK):
                    for nh in range(2):
                        S.op('act', lambda e, blk=blk, nh=nh: e.copy(out=candv[:, blk, nh * 512:(nh + 1) * 512], in_=banks[blk * 2 + nh][:, :]),
                             reads=[bbuf[blk * 2 + nh]], writes=list(b_cand))

            def epilogue(tg):
                tk0 = tg * TG
                lnp = lnp2
                blks = []
                for blk in range(NCK):
                    xt_, bx_, sc_ = ((x1t, b_x1t, ln_scratch), (x1t2, b_x1t2, ln_sc2))[blk % 2]
                    blks.append((blk, tk0 + blk * 128, xt_, bx_, sc_))

                def st_load():
                    for blk, r0, xt_, bx_, sc_ in blks:
                        S.dma('sp', None, lambda e, r0=r0, xt_=xt_: e.dma_start(out=xt_[:], in_=x1_scr[r0:r0 + 128, :]), reads=[b_scr], writes=[bx_])

                def st_res():
                    for blk, r0, xt_, bx_, sc_ in blks:
                        S.op('dve', lambda e, blk=blk, xt_=xt_: e.scalar_tensor_tensor(
                            out=xt_[:], in0=xt_[:], scalar=ALPHA, in1=candv[:, blk, :], op0=ALU.mult, op1=ALU.add),
                            reads=[bx_] + list(b_cand), writes=[bx_])

                def st_stats(c):
                    for blk, r0, xt_, bx_, sc_ in blks:
                        S.op('dve', lambda e, xt_=xt_, sc_=sc_: e.bn_stats(out=sc_[0][:, c, :], in_=xt_[:, c * 512:(c + 1) * 512]),
                             reads=[bx_], writes=[sc_[3]])

                def st_aggr():
                    for blk, r0, xt_, bx_, sc_ in blks:
                        S.op('dve', lambda e, sc_=sc_: e.bn_aggr(out=sc_[1][:], in_=sc_[0][:].rearrange("p c s -> p (c s)")), reads=[sc_[3]], writes=[sc_[3]])
                    for blk, r0, xt_, bx_, sc_ in blks:
                        S.op('act', lambda e, sc_=sc_: e.activation(out=sc_[2][:], in_=sc_[1][:, 1:2], func=AF.Sqrt, bias=eps_t[:], scale=1.0),
                             reads=[sc_[3], b_eps], writes=[sc_[3]])

                def st_norm():
                    for blk, r0, xt_, bx_, sc_ in blks:
                        S.op('dve', lambda e, sc_=sc_: e.reciprocal(out=sc_[2][:], in_=sc_[2][:]), reads=[sc_[3]], writes=[sc_[3]])
                    for blk, r0, xt_, bx_, sc_ in blks:
                        S.op('dve', lambda e, xt_=xt_, sc_=sc_: e.tensor_scalar(out=xt_[:], in0=xt_[:], scalar1=sc_[1][:, 0:1], scalar2=sc_[2][:],
                                                                              op0=ALU.subtract, op1=ALU.mult), reads=[bx_, sc_[3]], writes=[bx_])

                def st_gamma():
                    for blk, r0, xt_, bx_, sc_ in blks:
                        S.op('dve', lambda e, xt_=xt_: e.tensor_tensor(out=xt_[:], in0=xt_[:], in1=lnp[:, 0, :], op=ALU.mult),
                             reads=[bx_, b_lnp], writes=[bx_])

                def st_beta():
                    for blk, r0, xt_, bx_, sc_ in blks:
                        S.op('dve', lambda e, xt_=xt_: e.tensor_tensor(out=xt_[:], in0=xt_[:], in1=lnp[:, 1, :], op=ALU.add),
                             reads=[bx_, b_lnp], writes=[bx_])

                def stores():
                    for blk, r0, xt_, bx_, sc_ in blks:
                        S.dma('sp', None, lambda e, r0=r0, xt_=xt_: e.dma_start(out=out_d[r0:r0 + 128, :], in_=xt_[:]), reads=[bx_], writes=[b_out])
                stages = [st_load, st_res, lambda: st_stats(0), lambda: st_stats(1), st_aggr, st_norm, st_gamma, st_beta]
                return stages, stores

            def load_u(i2):
                k_ = i2 % NSB
                kcv = i2 // (128 // NCV)
                S.dma('sp', None, lambda e: e.dma_start(out=uT[k_][:], in_=Ub[i2]), reads=[b_Ub[kcv]], writes=[b_uT[k_]])

            def load_v(i2):
                k_ = i2 % NSB
                kcv = i2 // (128 // NCV)
                S.dma('sp', None, lambda e: e.dma_start(out=vS[k_][:], in_=Vb[i2]), reads=[b_Vb[kcv]], writes=[b_vS[k_]])

            for _, u_ in routing_units(0):
                u_()
            for tg in range(NG):
                tk0 = tg * TG
                for q_ in range(NSB):
                    load_u(q_)
                for q_ in range(NSB):
                    load_v(q_)
                if tg > 0:
                    evac_acc()
                gen_G(tg)
                nxt = routing_units(tg + 1) if tg + 1 < NG else []
                ui = 0

                def emit_A(i2, tk0=tk0):
                    k_ = i2 % NSB; a = i2 % 2; ab = (4, 5, 7)[i2 % 3]
                    for c in range(8):
                        mm(banks[ab][:, 0:TG], uT[k_][:, c, :], x1T[:, c, tk0:tk0 + TG], c == 0, c == 7, [b_uT[k_], b_x1T], [bbuf[ab]])
                    S.op('act', lambda e: e.activation(out=ga[a][:], in_=banks[ab][:, 0:TG], func=AF.Gelu),
                         reads=[bbuf[ab]], writes=[b_ga[a]])

                def emit_W(i2):
                    a = i2 % 2
                    S.op('pool', lambda e: e.tensor_tensor(out=WT[a][:], in0=ga[a][:], in1=Gt[:, :, i2], op=ALU.mult),
                         reads=[b_ga[a], b_G], writes=[b_WT[a]])

                def emit_out(i2):
                    k_ = i2 % NSB; a = i2 % 2
                    for blk in range(NCK):
                        for nh in range(2):
                            mm(banks[blk * 2 + nh][:, :], WT[a][:, blk * 128:(blk + 1) * 128], vS[k_][:, nh * 512:(nh + 1) * 512],
                               i2 == 0, i2 == 127, [b_WT[a], b_vS[k_]], [bbuf[blk * 2 + nh]])

                emit_A(0)
                emit_A(1)
                load_u(4)
                load_u(5)
                emit_W(0)
                for i2 in range(128):
                    if i2 + 2 < 128:
                        emit_A(i2 + 2)
                    if i2 + 1 < 128:
                        emit_W(i2 + 1)
                    emit_out(i2)
                    if i2 + 6 < 128:
                        load_u(i2 + 6)
                    if i2 + 4 < 128:
                        load_v(i2 + 4)
                    if i2 == 0 and tg > 0:
                        epi_stages, epi_stores = epilogue(tg - 1)
                    if tg > 0 and i2 >= 1 and epi_stages:
                        epi_stages.pop(0)()
                    if i2 == 22 and tg > 0:
                        epi_stores()
                    while ui < len(nxt) and nxt[ui][0] <= i2:
                        nxt[ui][1](); ui += 1
                while ui < len(nxt):
                    nxt[ui][1](); ui += 1
            evac_acc()
            fin_stages, fin_stores = epilogue(NG - 1)
            for st_ in fin_stages:
                st_()
            fin_stores()

        S.final_wait('sp', [b_out])
        S.final_wait('act', [b_out])
        S.run()
    return nc


def my_blocks(half):
    out = []
    for g in range(8):
        if half == 0:
            out += [4 * g, 4 * g + 3]
        else:
            out += [4 * g + 1, 4 * g + 2]
    return out


def make_masks(half):
    tri = (np.arange(128)[:, None] <= np.arange(128)[None, :]).astype(np.float32)
    ones = np.ones((128, 128), np.float32)
    zeros = np.zeros((128, 128), np.float32)
    if half == 0:
        m = [tri, zeros, ones, tri]
    else:
        m = [ones, tri, tri, zeros]
    return np.ascontiguousarray(np.stack(m, axis=1))


def prep_inputs(inputs):
    f = lambda a: np.ascontiguousarray(np.asarray(a, dtype=np.float32))
    x = f(inputs["x"])
    w_in = f(inputs["w_in"][0]); w_out = f(inputs["w_out"][0]); wq = f(inputs["peer_wq"][0])
    conv_w = f(inputs["conv_w"][0]); conv_b = f(inputs["conv_b"][0])
    cw = np.zeros((128, 4, 4), np.float32)
    for j in range(3):
        cw[:, :, j] = conv_w[j].reshape(4, 128).T
    cw[:, :, 3] = conv_b.reshape(4, 128).T
    lamv = np.stack([np.broadcast_to(f(inputs[k][0]), (128, 64)) for k in ("lam_q1", "lam_k1", "lam_q2", "lam_k2")], axis=1)
    sg = f(inputs["subln_g"][0]).reshape(128, 1)
    lnp = np.stack([np.broadcast_to(f(inputs[k][0]), (128, D)) for k in ("ln1_g", "ln1_b", "ln2_g", "ln2_b")], axis=1)
    keys = f(inputs["peer_keys"][0])
    keysT = np.ascontiguousarray(keys.reshape(16, 128, 128).transpose(2, 0, 1))
    u = f(inputs["peer_u"][0]).reshape(128, 128, 8, 128)
    U = np.ascontiguousarray(u.transpose(1, 3, 2, 0))
    v = f(inputs["peer_v"][0]).reshape(128, 128, D)
    V = np.ascontiguousarray(v.transpose(1, 0, 2))
    shared = {"w_in": w_in, "w_out": w_out, "cw": cw, "lamv": f(lamv), "sg": sg, "lnp": f(lnp), "wq": wq,
              "keysT": keysT, "U": U, "V": V}
    in_maps = []
    for core in range(8):
        b, half = core // 2, core % 2
        blks = my_blocks(half)
        tok = np.concatenate([np.arange(B * 128, (B + 1) * 128) for B in blks])
        xb = x[b]
        halo = np.zeros((32, D), np.float32)
        for m, B in enumerate(blks):
            if B > 0:
                halo[2 * m:2 * m + 2] = xb[B * 128 - 2:B * 128]
        m = dict(shared)
        m["xT_full"] = np.ascontiguousarray(xb.T)
        m["xT_mine"] = np.ascontiguousarray(xb[tok].T)
        m["xT_halo"] = np.ascontiguousarray(halo.T)
        m["x_mine"] = np.ascontiguousarray(xb[tok])
        m["masks"] = make_masks(half)
        in_maps.append(m)
    return in_maps


def assemble(results):
    out = np.zeros((4, SEQ, D), np.float32)
    for core in range(8):
        b, half = core // 2, core % 2
        blks = my_blocks(half)
        r = results[core]["out"]
        for m, B in enumerate(blks):
            out[b, B * 128:(B + 1) * 128] = r[m * 128:(m + 1) * 128]
    return out


_NC_CACHE = {}


def kernel(**inputs):
    in_maps = prep_inputs(inputs)
    if "nc" not in _NC_CACHE:
        _NC_CACHE["nc"] = build(stage=2)
    res = run_bass_kernel_spmd(_NC_CACHE["nc"], in_maps, core_ids=list(range(8)))
    return assemble(res.results)
```
